# Optimizing a Trainium2 kernel written in Bass

```python
import math
import jax, jax.numpy as jnp
from jax import lax
import numpy as np

D_MODEL = 2048
BATCH = 8
SEQ = 2048
DEPTH = 2

GRID_W = 64
CTX_LEN = 256

NA_HEADS = 8
NA_HEAD_DIM = 128
NA_WIN_ROWS = 8
NA_WIN_COLS = 16

RET_HEADS = 8
RET_KEY_DIM = 128
RET_VAL_DIM = 256
RET_CHUNK = 128

ROPE_BASE = 10000.0
ROPE_FREQS_PER_AXIS = RET_KEY_DIM // 4
NORM_EPS = 1e-6
MASK_VALUE = -1e30

W_NA = NA_HEADS * NA_HEAD_DIM
W_RET_QK = RET_HEADS * RET_KEY_DIM
W_RET_V = RET_HEADS * RET_VAL_DIM

NA_Q, NA_K, NA_V, NA_Z, RET_Q, RET_K, RET_V, RET_Z, G_NA, G_RET = range(10)
SPLIT_SIZES = (W_NA, W_NA, W_NA, W_NA, W_RET_QK, W_RET_QK, W_RET_V, W_RET_V, D_MODEL, D_MODEL)
SPLIT_OFFSETS = tuple(int(o) for o in np.cumsum((0,) + SPLIT_SIZES))
N_SPLITS = len(SPLIT_SIZES)
IN_COLS = SPLIT_OFFSETS[-1]

kernel_name = 'hybrid_na_retention_dit'


def rmsnorm(x, g):
    xf = x.astype(jnp.float32)
    y = xf * lax.rsqrt(jnp.mean(xf * xf, axis=-1, keepdims=True) + NORM_EPS)
    return (y * g.astype(jnp.float32)).astype(x.dtype)


def to_heads(t, n_heads):
    b, n, w = t.shape
    return t.reshape(b, n, n_heads, w // n_heads).transpose(0, 2, 1, 3)


def from_heads(t):
    b, h, n, d = t.shape
    return t.transpose(0, 2, 1, 3).reshape(b, n, h * d)


def in_block(t, i):
    return t[..., SPLIT_OFFSETS[i]:SPLIT_OFFSETS[i + 1]]


def axial_rope(n_tokens, dtype):
    t = jnp.arange(n_tokens)
    row = (t // GRID_W).astype(jnp.float32)
    col = (t % GRID_W).astype(jnp.float32)
    inv_freq = ROPE_BASE ** (-jnp.arange(ROPE_FREQS_PER_AXIS, dtype=jnp.float32) / ROPE_FREQS_PER_AXIS)
    ang = jnp.concatenate([row[:, None] * inv_freq, col[:, None] * inv_freq], axis=-1)
    return jnp.cos(ang).astype(dtype), jnp.sin(ang).astype(dtype)


def apply_rope(x, cos, sin):
    half = x.shape[-1] // 2
    x1, x2 = x[..., :half], x[..., half:]
    return jnp.concatenate([x1 * cos - x2 * sin, x2 * cos + x1 * sin], axis=-1)


def neighbourhood_attention(q, k, v, k_ctx, v_ctx, rpb):
    b, h, s, d = q.shape
    rows = s // GRID_W
    kh = min(NA_WIN_ROWS, rows)
    kw = NA_WIN_COLS
    r = jnp.arange(rows)
    cidx = jnp.arange(GRID_W)
    r0 = jnp.clip(r - kh // 2, 0, rows - kh)
    row_idx = r0[:, None] + jnp.arange(kh)[None, :]
    c0 = jnp.clip(cidx - kw // 2, 0, GRID_W - kw)
    col_in = (cidx[None, :] >= c0[:, None]) & (cidx[None, :] < c0[:, None] + kw)
    qg = q.reshape(b, h, rows, GRID_W, d)
    kg = jnp.take(k.reshape(b, h, rows, GRID_W, d), row_idx, axis=2)
    vg = jnp.take(v.reshape(b, h, rows, GRID_W, d), row_idx, axis=2)
    scale = d ** -0.5
    s_loc = jnp.einsum('bhrcd,bhrkwd->bhrckw', qg, kg).astype(jnp.float32) * scale
    dr = row_idx - r[:, None] + (NA_WIN_ROWS - 1)
    dc = jnp.clip(cidx[None, :] - cidx[:, None] + (NA_WIN_COLS - 1), 0, 2 * NA_WIN_COLS - 2)
    bias = rpb[:, dr[:, None, :, None], dc[None, :, None, :]].astype(jnp.float32)
    s_loc = jnp.where(col_in[None, None, None, :, None, :], s_loc + bias[None], MASK_VALUE)
    s_ctx = jnp.einsum('bhrcd,bhld->bhrcl', qg, k_ctx).astype(jnp.float32) * scale
    n_loc = kh * GRID_W
    scores = jnp.concatenate([s_loc.reshape(b, h, rows, GRID_W, n_loc), s_ctx], axis=-1)
    p = jax.nn.softmax(scores, axis=-1).astype(v.dtype)
    p_loc = p[..., :n_loc].reshape(b, h, rows, GRID_W, kh, GRID_W)
    p_ctx = p[..., n_loc:]
    out = (jnp.einsum('bhrckw,bhrkwd->bhrcd', p_loc, vg)
           + jnp.einsum('bhrcl,bhld->bhrcd', p_ctx, v_ctx))
    return out.reshape(b, h, s, d)


def context_attention(q, k, v):
    s = jnp.einsum('bhqd,bhkd->bhqk', q, k).astype(jnp.float32) * (q.shape[-1] ** -0.5)
    p = jax.nn.softmax(s, axis=-1).astype(v.dtype)
    return jnp.einsum('bhqk,bhkd->bhqd', p, v)


def retention_chunkwise(q, k, v, log_gamma, state0):
    b, h, n, dk = q.shape
    dv = v.shape[-1]
    cs = RET_CHUNK
    nc = n // cs
    pos = jnp.arange(cs, dtype=jnp.float32)
    diff = pos[:, None] - pos[None, :]
    intra = jnp.where(diff >= 0, jnp.exp(jnp.maximum(diff, 0.0) * log_gamma[:, None, None]), 0.0)
    q_dec = jnp.exp((pos + 1.0)[None, :] * log_gamma[:, None])
    k_dec = jnp.exp((cs - 1.0 - pos)[None, :] * log_gamma[:, None])
    chunk_dec = jnp.exp(cs * log_gamma)
    qc = q.reshape(b, h, nc, cs, dk)
    kc = k.reshape(b, h, nc, cs, dk)
    vc = v.reshape(b, h, nc, cs, dv)
    scores = jnp.einsum('bhnid,bhnjd->bhnij', qc, kc) * intra[None, :, None]
    inner = jnp.einsum('bhnij,bhnje->bhnie', scores, vc)

    def step(state, xs):
        q_i, k_i, v_i = xs
        cross = jnp.einsum('bhid,bhde->bhie', q_i * q_dec[None, :, :, None], state)
        state = (state * chunk_dec[None, :, None, None]
                 + jnp.einsum('bhjd,bhje->bhde', k_i * k_dec[None, :, :, None], v_i))
        return state, cross

    xs = (jnp.moveaxis(qc, 2, 0), jnp.moveaxis(kc, 2, 0), jnp.moveaxis(vc, 2, 0))
    state_final, cross = lax.scan(step, state0, xs)
    out = inner + jnp.moveaxis(cross, 0, 2)
    return out.reshape(b, h, n, dv), state_final


def bidir_retention(q, k, v, log_gamma, state_fwd, state_bwd):
    q, k, v = q.astype(jnp.float32), k.astype(jnp.float32), v.astype(jnp.float32)
    o_f, s_f = retention_chunkwise(q, k, v, log_gamma[0], state_fwd)
    o_b, s_b = retention_chunkwise(jnp.flip(q, 2), jnp.flip(k, 2), jnp.flip(v, 2), log_gamma[1], state_bwd)
    return o_f + jnp.flip(o_b, 2), s_f, s_b


def context_final_states(k, v, log_gamma):
    k, v = k.astype(jnp.float32), v.astype(jnp.float32)
    n = k.shape[2]
    pos = jnp.arange(n, dtype=jnp.float32)
    w_f = jnp.exp((n - 1.0 - pos)[None, :] * log_gamma[0][:, None])
    w_b = jnp.exp(pos[None, :] * log_gamma[1][:, None])
    s_f = jnp.einsum('bhld,bhle->bhde', k * w_f[None, :, :, None], v)
    s_b = jnp.einsum('bhld,bhle->bhde', k * w_b[None, :, :, None], v)
    return s_f, s_b


def ret_head_norm(o, dtype):
    o = o * lax.rsqrt(jnp.mean(o * o, axis=-1, keepdims=True) + NORM_EPS)
    return o.astype(dtype)


def merge_branches(o_na, o_ret, blocks, w_proj_na, w_proj_ret, w_out):
    dtype = blocks[NA_Z].dtype
    y_na = (from_heads(o_na) * jax.nn.silu(blocks[NA_Z])) @ w_proj_na
    y_ret = (from_heads(ret_head_norm(o_ret, dtype)) * jax.nn.silu(blocks[RET_Z])) @ w_proj_ret
    merged = jax.nn.sigmoid(blocks[G_NA]) * y_na + jax.nn.sigmoid(blocks[G_RET]) * y_ret
    return merged @ w_out


def hybrid_layer(x_lat, x_ctx, mod_lat, mod_ctx, norm_g, w_in, rpb, decay_logit,
                 w_proj_na, w_proj_ret, w_out, update_ctx):
    shift, scale, gate = jnp.split(mod_lat, 3, axis=-1)
    c_shift, c_scale, c_gate = jnp.split(mod_ctx, 3, axis=-1)
    h_lat = rmsnorm(x_lat, norm_g) * (1.0 + scale[:, None]) + shift[:, None]
    h_ctx = rmsnorm(x_ctx, norm_g) * (1.0 + c_scale) + c_shift
    log_gamma = jax.nn.log_sigmoid(decay_logit.astype(jnp.float32))

    u = h_lat @ w_in
    lat = {i: in_block(u, i) for i in range(N_SPLITS)}
    if update_ctx:
        uc = h_ctx @ w_in
        cb = {i: in_block(uc, i) for i in range(N_SPLITS)}
    else:
        cb = {i: h_ctx @ in_block(w_in, i) for i in (NA_K, NA_V, RET_K, RET_V)}

    k_na_ctx = to_heads(cb[NA_K], NA_HEADS)
    v_na_ctx = to_heads(cb[NA_V], NA_HEADS)
    o_na = neighbourhood_attention(to_heads(lat[NA_Q], NA_HEADS), to_heads(lat[NA_K], NA_HEADS),
                                   to_heads(lat[NA_V], NA_HEADS), k_na_ctx, v_na_ctx, rpb)

    k_scale = RET_KEY_DIM ** -0.5
    cos, sin = axial_rope(x_lat.shape[1], x_lat.dtype)
    q_ret = apply_rope(to_heads(lat[RET_Q], RET_HEADS), cos, sin)
    k_ret = apply_rope(to_heads(lat[RET_K], RET_HEADS), cos, sin) * k_scale
    v_ret = to_heads(lat[RET_V], RET_HEADS)
    k_ret_ctx = to_heads(cb[RET_K], RET_HEADS) * k_scale
    v_ret_ctx = to_heads(cb[RET_V], RET_HEADS)
    if update_ctx:
        b = x_ctx.shape[0]
        zeros = jnp.zeros((b, RET_HEADS, RET_KEY_DIM, RET_VAL_DIM), jnp.float32)
        o_ret_ctx, s_f, s_b = bidir_retention(to_heads(cb[RET_Q], RET_HEADS), k_ret_ctx, v_ret_ctx,
                                              log_gamma, zeros, zeros)
    else:
        s_f, s_b = context_final_states(k_ret_ctx, v_ret_ctx, log_gamma)
    o_ret, _, _ = bidir_retention(q_ret, k_ret, v_ret, log_gamma, s_f, s_b)

    out_lat = merge_branches(o_na, o_ret, lat, w_proj_na, w_proj_ret, w_out)
    x_lat = x_lat + gate[:, None] * out_lat
    if update_ctx:
        o_na_ctx = context_attention(to_heads(cb[NA_Q], NA_HEADS), k_na_ctx, v_na_ctx)
        out_ctx = merge_branches(o_na_ctx, o_ret_ctx, cb, w_proj_na, w_proj_ret, w_out)
        x_ctx = x_ctx + c_gate * out_ctx
    return x_lat, x_ctx


def setup_inputs(seed: int = 0) -> dict:
    key = jax.random.key(seed)
    ks = jax.random.split(key, 14)
    f32 = jnp.float32
    base_logit = jnp.log(2.0 ** (5.0 + jnp.arange(RET_HEADS, dtype=f32)) - 1.0)
    return {
        'x': jax.random.normal(ks[0], (BATCH, SEQ, D_MODEL), f32),
        'c': jax.random.normal(ks[1], (BATCH, D_MODEL), f32),
        'ctx': jax.random.normal(ks[2], (BATCH, CTX_LEN, D_MODEL), f32),
        'c_ctx': jax.random.normal(ks[3], (D_MODEL,), f32),
        'ada_w': jax.random.normal(ks[4], (DEPTH, D_MODEL, 3 * D_MODEL), f32) * D_MODEL ** -0.5,
        'ada_b': jax.random.normal(ks[5], (DEPTH, 3 * D_MODEL), f32) * 0.01,
        'norm_g': 1.0 + 0.1 * jax.random.normal(ks[6], (DEPTH, D_MODEL), f32),
        'w_in': jax.random.normal(ks[7], (DEPTH, D_MODEL, IN_COLS), f32) * D_MODEL ** -0.5,
        'na_rpb': 0.1 * jax.random.normal(ks[8], (DEPTH, NA_HEADS, 2 * NA_WIN_ROWS - 1, 2 * NA_WIN_COLS - 1), f32),
        'ret_decay_logit': base_logit[None, None, :] + 0.1 * jax.random.normal(ks[9], (DEPTH, 2, RET_HEADS), f32),
        'w_proj_na': jax.random.normal(ks[10], (DEPTH, W_NA, D_MODEL), f32) * W_NA ** -0.5,
        'w_proj_ret': jax.random.normal(ks[11], (DEPTH, W_RET_V, D_MODEL), f32) * W_RET_V ** -0.5,
        'w_out': jax.random.normal(ks[12], (DEPTH, D_MODEL, D_MODEL), f32) * D_MODEL ** -0.5,
        'final_g': 1.0 + 0.1 * jax.random.normal(ks[13], (D_MODEL,), f32),
    }


def reference(x, c, ctx, c_ctx, ada_w, ada_b, norm_g, w_in, na_rpb, ret_decay_logit,
              w_proj_na, w_proj_ret, w_out, final_g):
    c_silu = jax.nn.silu(c)
    cc_silu = jax.nn.silu(c_ctx)
    x_lat, x_ctx = x, ctx
    for l in range(DEPTH):
        mod_lat = c_silu @ ada_w[l] + ada_b[l]
        mod_ctx = cc_silu @ ada_w[l] + ada_b[l]
        x_lat, x_ctx = hybrid_layer(x_lat, x_ctx, mod_lat, mod_ctx, norm_g[l], w_in[l], na_rpb[l],
                                    ret_decay_logit[l], w_proj_na[l], w_proj_ret[l], w_out[l],
                                    update_ctx=(l < DEPTH - 1))
    return rmsnorm(x_lat, final_g)
```

```python
import contextlib
import numpy as np
import concourse.bass as bass
import concourse.mybir as mybir
from concourse.bass_utils import run_bass_kernel_spmd

F32 = mybir.dt.float32
BF16 = mybir.dt.bfloat16
AF = mybir.ActivationFunctionType
ALU = mybir.AluOpType

D = 2048
KC = 16
TL = 2048
TC = 256
T = TL + TC
NTT = T // 128
IN_COLS = 14336
NH = 8
EPS = 1e-6
NEG = -30000.0
DEPTH = 2
EV_ACT_ONLY = False
POOL_ENG = 'dve'
RET_SKEW = [0, 2, 4]
NA_SKEW = [0, 0, 3, 3]

O_NAQ, O_NAK, O_NAV, O_NAZ = 0, 1024, 2048, 3072
O_RQ, O_RK, O_RV, O_RZ = 4096, 5120, 6144, 8192
O_GNA, O_GRET = 10240, 12288


class Buf:
    def __init__(self, ap, name, coarse=False):
        self.ap = ap
        self.name = name
        self.coarse = coarse
        self.writers = {}
        self.readers = {}
        self.dsem = None
        self.dcnt = 0


class KB:
    def __init__(self, nc, es):
        self.nc = nc
        self.es = es
        self.eng = {"pe": nc.tensor, "act": nc.scalar, "dve": nc.vector,
                    "pool": nc.gpsimd, "sp": nc.sync}
        self.sems = {}
        self.cnt = {}
        for n in ("pe", "act", "dve", "pool"):
            self.sems[n] = es.enter_context(nc.semaphore("s_" + n))
            self.cnt[n] = 0
        self.waited = {n: {} for n in self.eng}
        self.pe_pending = []
        self.nd = 0
        self.free_dsems = {"sw": [], "hw": []}

    def sbuf(self, es, name, shape, dtype, coarse=False):
        self.nalloc = getattr(self, "nalloc", 0) + 1
        name = "sb%d_%s" % (self.nalloc, name)
        t = es.enter_context(self.nc.sbuf_tensor(name, list(shape), dtype))
        return Buf(t, name, coarse)

    def psum(self, es, name, shape, dtype):
        t = es.enter_context(self.nc.psum_tensor(name, list(shape), dtype))
        return Buf(t, name)

    def dram(self, name, shape, dtype, kind="Internal"):
        t = self.nc.dram_tensor(name, list(shape), dtype, kind=kind).ap()
        return Buf(t, name, coarse=True)

    def _wait(self, e, toks):
        w = self.waited[e]
        for key, val in toks.items():
            if w.get(key, 0) < val:
                self.eng[e].wait_ge(self.sems[key], val)
                w[key] = val

    @staticmethod
    def _merge(dst, src):
        for k, v in src.items():
            if dst.get(k, 0) < v:
                dst[k] = v

    def _deps(self, reads, writes):
        d = {}
        for b in reads:
            self._merge(d, b.writers)
        for b in writes:
            self._merge(d, b.writers)
            self._merge(d, b.readers)
        return d

    def _mark(self, reads, writes, key, val):
        for b in reads:
            if b.readers.get(key, 0) < val:
                b.readers[key] = val
        for b in writes:
            if b.coarse:
                if b.writers.get(key, 0) < val:
                    b.writers[key] = val
            else:
                b.writers = {key: val}
                b.readers = {}

    def op(self, e, fn, reads=(), writes=(), signal=True):
        self._wait(e, self._deps(reads, writes))
        ins = fn(self.eng[e])
        if e == "pe" and not signal:
            self.pe_pending.append((list(reads), list(writes)))
            return ins
        self.cnt[e] += 1
        ins.then_inc(self.sems[e], 1)
        self._mark(reads, writes, e, self.cnt[e])
        if e == "pe":
            for r, w in self.pe_pending:
                self._mark(r, w, e, self.cnt[e])
            self.pe_pending = []
        return ins

    def dma(self, q, out_buf, out_ap, in_buf, in_ap, sem_buf=None, **kw):
        sb = sem_buf
        if sb is None:
            sb = out_buf if not out_buf.coarse else in_buf
        cls = "sw" if q == "pool" else "hw"
        if sb.dsem is None:
            sb.dsem = {}
        if cls not in sb.dsem:
            if self.free_dsems[cls]:
                sb.dsem[cls] = list(self.free_dsems[cls].pop())
            else:
                self.nd += 1
                key = "d%d" % self.nd
                self.sems[key] = self.es.enter_context(self.nc.semaphore(key))
                sb.dsem[cls] = [key, 0]
        ent = sb.dsem[cls]
        self._wait(q, self._deps([in_buf], [out_buf]))
        ins = self.eng[q].dma_start(out=out_ap, in_=in_ap, **kw)
        ent[1] += 16
        ins.then_inc(self.sems[ent[0]], 16)
        self._mark([in_buf], [out_buf], ent[0], ent[1])
        return ins

    def release(self, stack):
        for b in getattr(stack, "_bufs", []):
            if b.dsem:
                for cls, ent in b.dsem.items():
                    self.free_dsems[cls].append(tuple(ent))
                b.dsem = None

    def barrier(self):
        allt = {n: self.cnt[n] for n in ("pe", "act", "dve", "pool") if self.cnt[n] > 0}
        assert not self.pe_pending
        for b in self._all_bufs:
            if b.dsem:
                for ent in b.dsem.values():
                    allt[ent[0]] = ent[1]
        for cls in self.free_dsems:
            for key, val in self.free_dsems[cls]:
                allt[key] = val
        for e in self.eng:
            self._wait(e, allt)


class Pool:
    def __init__(self, bufs):
        self.bufs = bufs
        self.i = 0

    def next(self):
        b = self.bufs[self.i % len(self.bufs)]
        self.i += 1
        return b


def _na_patterns():
    rows, W = 32, 64
    kh, kw = 8, 16
    pats = {}
    pat_of = {}
    drs, dcs, vs = [], [], []
    kk = np.arange(128)
    krow_l, kcol = kk // 64, kk % 64
    for qb in range(16):
        qrow = 2 * qb + krow_l
        qcol = kcol
        r0 = np.clip(qrow - kh // 2, 0, rows - kh)
        c0 = np.clip(qcol - kw // 2, 0, W - kw)
        for kt in range(16):
            krow = 2 * kt + krow_l
            rv = (krow[:, None] >= r0[None, :]) & (krow[:, None] < r0[None, :] + kh)
            cv = (kcol[:, None] >= c0[None, :]) & (kcol[:, None] < c0[None, :] + kw)
            valid = rv & cv
            if not valid.any():
                continue
            dr = np.clip(krow[:, None] - qrow[None, :] + 7, 0, 14)
            dc = np.clip(kcol[:, None] - qcol[None, :] + 15, 0, 30)
            dr = np.where(valid, dr, 0).astype(np.int32)
            dc = np.where(valid, dc, 0).astype(np.int32)
            key = (dr.tobytes(), dc.tobytes(), valid.tobytes())
            if key not in pats:
                pats[key] = len(drs)
                drs.append(dr); dcs.append(dc); vs.append(valid)
            pat_of[(qb, kt)] = pats[key]
    return pat_of, np.stack(drs), np.stack(dcs), np.stack(vs)


_PAT_OF, _PDR, _PDC, _PV = _na_patterns()
NPAT = _PDR.shape[0]


def _rope_tables():
    t = np.arange(TL)
    row = (t // 64).astype(np.float32)
    col = (t % 64).astype(np.float32)
    nf = 32
    inv = (np.float32(10000.0) ** (-np.arange(nf, dtype=np.float32) / np.float32(nf))).astype(np.float32)
    ang = np.concatenate([row[:, None] * inv, col[:, None] * inv], axis=-1).astype(np.float32)
    cos = np.cos(ang).astype(np.float32).T
    sin = np.sin(ang).astype(np.float32).T
    C = np.concatenate([cos, cos], axis=0)
    S = np.concatenate([-sin, sin], axis=0)
    return np.ascontiguousarray(C), np.ascontiguousarray(S)


def _ret_pos_tables():
    p = np.arange(128, dtype=np.float32)
    key = p[:, None]
    qry = p[None, :]
    tabs = np.zeros((128, 7, 128), np.float32)
    tabs[:, 0, :] = np.maximum(qry - key, 0)
    tabs[:, 1, :] = np.maximum(key - qry, 0)
    tabs[:, 2, :] = (key < qry).astype(np.float32)
    tabs[:, 3, :] = (key > qry).astype(np.float32)
    tabs[:, 4, :] = 2.0 * (key == qry)
    tabs[:, 5, :] = np.broadcast_to(qry + 1.0, (128, 128))
    tabs[:, 6, :] = np.broadcast_to(128.0 - qry, (128, 128))
    cols = np.zeros((128, 2), np.float32)
    cols[:, 0] = 127.0 - p
    cols[:, 1] = p
    return tabs, cols


def build_program(dump=(), layers=(0, 1), stop_after=None):
    nc = bass.Bass("TRN2", target_bir_lowering=False)
    es = contextlib.ExitStack()
    with es:
        k = KB(nc, es)
        k._all_bufs = []
        _build(nc, es, k, dump, layers, stop_after)
    return nc


def _build(nc, es, k, dump, layers, stop_after):
    allb = k._all_bufs

    def reg(b):
        allb.append(b)
        return b

    def din(name, shape, dtype=F32):
        return reg(k.dram(name, shape, dtype, kind="ExternalInput"))

    def dscr(name, shape, dtype):
        kind = "ExternalOutput" if name in dump else "Internal"
        return reg(k.dram(name, shape, dtype, kind=kind))

    def sb(stack, name, shape, dtype, coarse=False):
        b = reg(k.sbuf(stack, name, shape, dtype, coarse))
        if not hasattr(stack, "_bufs"):
            stack._bufs = []
        stack._bufs.append(b)
        return b

    xin = din("xin", [T, D])
    cT = din("cT", [128, 32])
    ada_w = din("ada_w", [DEPTH, D, 3 * D])
    adabT = din("adabT", [128, DEPTH * 48])
    gT_in = din("gT", [128, DEPTH * KC])
    w_in = din("w_in", [DEPTH, D, IN_COLS])
    natab = din("natab", [DEPTH, NH, 128, NPAT, 128])
    dlog = din("dlog", [128, DEPTH * 16])
    w_pna = din("w_pna", [DEPTH, 1024, D])
    w_pret = din("w_pret", [DEPTH, 2048, D])
    w_out = din("w_out", [DEPTH, D, D])
    fg_in = din("final_g", [1, D])
    ident_in = din("ident", [128, 128])
    perm_in = din("perm", [128, 128])
    ropeC_in = din("ropeC", [128, TL])
    ropeS_in = din("ropeS", [128, TL])
    rpos_in = din("rpos", [128, 7, 128])
    rcol_in = din("rcol", [128, 2])
    yout = reg(k.dram("yout", [TL, D], F32, kind="ExternalOutput"))

    QT_na = dscr("QT_na", [1024, T], BF16)
    KT_na = dscr("KT_na", [1024, T], BF16)
    V_na = dscr("V_na", [T, 1024], BF16)
    ZT_na = dscr("ZT_na", [1024, T], BF16)
    QT_r = dscr("QT_r", [1024, T], BF16)
    KT_r = dscr("KT_r", [1024, T], BF16)
    V_r = dscr("V_r", [T, 2048], BF16)
    ZT_r = dscr("ZT_r", [2048, T], BF16)
    SG = dscr("SG", [4096, T], BF16)
    AT = dscr("AT", [3072, T], BF16)
    MT = dscr("MT", [2048, T], BF16)
    X1 = dscr("X1", [T, D], F32)
    GV = dscr("GV", [DEPTH * 2, D], F32)
    HTd = dscr("HTd", [D, T], BF16) if "HTd" in dump else None

    ident = sb(es, "ident", [128, 128], F32)
    identb = sb(es, "identb", [128, 128], BF16)
    onesb = sb(es, "onesb", [128, 128], BF16)
    modT = sb(es, "modT", [128, DEPTH, 48, 2], F32)
    gTt = sb(es, "gTt", [128, DEPTH * KC], F32)
    psb = [reg(k.psum(es, "ps%d" % i, [128, 512], F32)) for i in range(8)]
    PS = Pool(psb[:7])
    psM = psb[7]
    epsc = sb(es, "epsc", [128, 1], F32)
    k.op("dve", lambda e: e.memset(epsc.ap[:], EPS), [], [epsc])

    def rstd_ops(ss, inv_n):
        k.op("act", lambda e: e.activation(out=ss.ap[:, 1:2], in_=ss.ap[:, 0:1], func=AF.Sqrt,
                                           scale=inv_n, bias=epsc.ap[:, 0:1]), [ss, epsc], [ss])
        k.op("dve", lambda e: e.reciprocal(out=ss.ap[:, 1:2], in_=ss.ap[:, 1:2]), [ss], [ss])

    k.dma("sp", ident, ident.ap[:], ident_in, ident_in.ap)
    permf = sb(es, "permf", [128, 128], F32)
    permb = sb(es, "permb", [128, 128], BF16)
    k.dma("sp", permf, permf.ap[:], perm_in, perm_in.ap)
    k.op("dve", lambda e: e.tensor_copy(out=permb.ap[:], in_=permf.ap[:]), [permf], [permb])
    k.dma("sp", gTt, gTt.ap[:], gT_in, gT_in.ap)
    k.op("dve", lambda e: e.tensor_copy(out=identb.ap[:], in_=ident.ap[:]), [ident], [identb])
    k.op("dve", lambda e: e.memset(onesb.ap[:], 1.0), [], [onesb])

    mstack = contextlib.ExitStack()
    HALF = 3 * D // 2
    sc = sb(mstack, "sc", [128, 32], F32)
    adab = sb(mstack, "adab", [128, DEPTH * 48], F32)
    wa = Pool([sb(mstack, "wa%d" % i, [128, HALF], F32) for i in range(2)])
    wb = Pool([sb(mstack, "wb%d" % i, [128, HALF], BF16) for i in range(2)])
    scb = sb(mstack, "scb", [128, 32], BF16)
    k.dma("sp", sc, sc.ap[:], cT, cT.ap)
    k.dma("sp", adab, adab.ap[:], adabT, adabT.ap)
    k.op("act", lambda e: e.activation(out=sc.ap[:], in_=sc.ap[:], func=AF.Silu), [sc], [sc])
    k.op("dve", lambda e: e.tensor_copy(out=scb.ap[:], in_=sc.ap[:]), [sc], [scb])
    sc3 = scb.ap[:].rearrange("p (r k) -> p k r", r=2)
    mst8 = {}
    NM = 2 * KC

    def m_begin(l):
        k.op("dve", lambda e: e.memset(psM.ap[:], 0.0), [], [psM])

    def m_load(l, i):
        kc, hf = divmod(i, 2)
        w = wa.next()
        k.dma("sp", w, w.ap[:], ada_w, ada_w.ap[l, kc * 128:(kc + 1) * 128, hf * HALF:(hf + 1) * HALF])
        mst8[(l, i)] = w

    def m_compute(l, i, cast_eng):
        kc, hf = divmod(i, 2)
        w = mst8.pop((l, i))
        wbf = wb.next()
        if cast_eng == "act":
            k.op("act", lambda e: e.copy(out=wbf.ap[:], in_=w.ap[:]), [w], [wbf])
        else:
            k.op("dve", lambda e: e.tensor_copy(out=wbf.ap[:], in_=w.ap[:]), [w], [wbf])
        for g in range(24):
            gg = hf * 24 + g
            k.op("pe", lambda e, g=g, gg=gg: e.matmul(
                psM.ap[:, 2 * gg:2 * gg + 2], lhsT=wbf.ap[:, g * 128:(g + 1) * 128],
                rhs=sc3[:, kc, :], start=False, stop=(kc == KC - 1), skip_group_check=True),
                [wbf, scb], [psM], signal=(g == 23))

    def m_end(l):
        bias_bc = adab.ap[:, l * 48:(l + 1) * 48].unsqueeze(2).to_broadcast([128, 48, 2])
        k.op("dve", lambda e: e.tensor_tensor(
            out=modT.ap[:, l], in0=psM.ap[:, 0:96].rearrange("p (g r) -> p g r", r=2),
            in1=bias_bc, op=ALU.add), [psM, adab], [modT])
        for r in range(2):
            with nc.allow_non_contiguous_dma(reason="tiny gate vector transpose"):
                k.dma("sp", GV, GV.ap[l * 2 + r].rearrange("(k p) -> p k", p=128),
                      modT, modT.ap[:, l, 32:48, r], sem_buf=modT)

    defer_m = (len(layers) == 2 and stop_after is None)
    for l in (layers[:1] if defer_m else layers):
        m_begin(l)
        m_load(l, 0)
        for i in range(NM):
            if i + 1 < NM:
                m_load(l, i + 1)
            m_compute(l, i, "act" if i % 2 == 0 else "dve")
        m_end(l)
    k.barrier()
    if not defer_m:
        k.release(mstack)
        mstack.close()
        PS.bufs = list(psb)
    mi = [0]

    def m_tick(n):
        for _ in range(n):
            i = mi[0]
            if i >= NM:
                return
            if i == 0:
                m_begin(layers[1])
                m_load(layers[1], 0)
            if i + 1 < NM:
                m_load(layers[1], i + 1)
            m_compute(layers[1], i, "act")
            mi[0] += 1

    if stop_after == "M":
        _finish(nc, k, yout, modT, dump)
        return

    for l in layers:
        lay12 = contextlib.ExitStack()
        hT = sb(lay12, "hT", [128, KC, T], BF16, coarse=True)
        hTa = reg(Buf(hT.ap, "hTa", coarse=True))
        hTb = reg(Buf(hT.ap, "hTb", coarse=True))
        wt_pool = Pool([sb(lay12, "wt%d" % i, [128, KC, 512], BF16) for i in range(2)])
        pre_wt = {}
        for cc in (0, 512):
            wt = wt_pool.next()
            k.dma("pool", wt, wt.ap[:], w_in,
                  w_in.ap[l, :, cc:cc + 512].rearrange("(k p) c -> p k c", p=128))
            pre_wt[cc] = wt
        last = (l == DEPTH - 1)
        ntt = NTT if True else 16
        xsrc = xin if l == 0 else X1
        with contextlib.ExitStack() as ph:
            gm = sb(ph, "gm", [128, KC, 2], F32)
            xt_pool = Pool([sb(ph, "xt%d" % i, [128, D], F32) for i in range(3)])
            junk = sb(ph, "junk", [128, D], BF16)
            ssp = Pool([sb(ph, "ss%d" % i, [128, 2], F32) for i in range(3)])
            k.op("dve", lambda e: e.tensor_scalar(out=gm.ap[:], in0=modT.ap[:, l, 16:32, :],
                                                  scalar1=1.0, scalar2=None, op0=ALU.add),
                 [modT], [gm])
            k.op("dve", lambda e: e.tensor_tensor(
                out=gm.ap[:], in0=gm.ap[:],
                in1=gTt.ap[:, l * KC:(l + 1) * KC].unsqueeze(2).to_broadcast([128, KC, 2]),
                op=ALU.mult), [gm, gTt], [gm])
            p1 = [dict() for _ in range(ntt)]

            def p1a(t):
                xt = xt_pool.next()
                ss = ssp.next()
                k.dma("sp", xt, xt.ap[:], xsrc, xsrc.ap[t * 128:(t + 1) * 128, :])
                k.op("act", lambda e: e.activation(out=junk.ap[:], in_=xt.ap[:], func=AF.Square,
                                                   accum_out=ss.ap[:, 0:1]), [xt], [junk, ss])
                rstd_ops(ss, 1.0 / D)
                k.op("dve", lambda e: e.tensor_scalar(out=xt.ap[:], in0=xt.ap[:], scalar1=ss.ap[:, 1:2],
                                                      scalar2=None, op0=ALU.mult), [xt, ss], [xt])
                p1[t]["xt"] = xt

            def p1b(t):
                r = 0 if t < 16 else 1
                xt = p1[t]["xt"]
                for q4 in range(4):
                    pt = PS.next()
                    for j in range(4):
                        kc = q4 * 4 + j
                        k.op("pe", lambda e, kc=kc, j=j: e.transpose(
                            out=pt.ap[:, j * 128:(j + 1) * 128], in_=xt.ap[:, kc * 128:(kc + 1) * 128],
                            identity=ident.ap[:]), [xt, ident], [pt], signal=(j == 3))
                    for j in range(4):
                        kc = q4 * 4 + j
                        dst = hT.ap[:, kc, t * 128:(t + 1) * 128]
                        src = pt.ap[:, j * 128:(j + 1) * 128]
                        if q4 < 2:
                            k.op("act", lambda e, dst=dst, src=src, kc=kc: e.activation(
                                out=dst, in_=src, func=AF.Identity,
                                scale=gm.ap[:, kc, r:r + 1], bias=modT.ap[:, l, kc, r:r + 1]),
                                [pt, gm, modT], [hTa])
                        else:
                            k.op("dve", lambda e, dst=dst, src=src, kc=kc: e.tensor_scalar(
                                out=dst, in0=src, scalar1=gm.ap[:, kc, r:r + 1],
                                scalar2=modT.ap[:, l, kc, r:r + 1], op0=ALU.mult, op1=ALU.add),
                                [pt, gm, modT], [hTb])

            _pipeline([p1a, p1b], [0, 1], ntt)
            if HTd is not None:
                for kc in range(KC):
                    k.dma("sp", HTd, HTd.ap[kc * 128:(kc + 1) * 128, :], hTa if kc < 8 else hTb, hT.ap[:, kc, :])
            k.barrier()
            k.release(ph)
        if stop_after == "1":
            lay12.close()
            break

        with contextlib.ExitStack() as ph:
            pbf_pool = Pool([sb(ph, "pbf%d" % i, [128, 512], BF16) for i in range(3)])
            deferred = []
            stg_pool = Pool([sb(ph, "stg%d" % i, [128, T], BF16) for i in range(3)])
            stv_pool = Pool([sb(ph, "stv%d" % i, [128, 512], BF16) for i in range(3)])
            ropeC = sb(ph, "ropeC", [128, TL], F32)
            ropeS = sb(ph, "ropeS", [128, TL], F32)
            t1p = Pool([sb(ph, "t1_%d" % i, [128, 512], F32) for i in range(2)])
            t2p = Pool([sb(ph, "t2_%d" % i, [128, 512], F32) for i in range(2)])
            k.dma("sp", ropeC, ropeC.ap[:], ropeC_in, ropeC_in.ap)
            k.dma("sp", ropeS, ropeS.ap[:], ropeS_in, ropeS_in.ap)

            lat_blocks = [(i * 512, 512) for i in range(4)]
            ctx_block = [(TL, TC)]
            specs = [
                (O_NAQ, 1024, "q", QT_na, not last),
                (O_NAK, 1024, "copy", KT_na, True),
                (O_NAV, 1024, "tok", V_na, True),
                (O_NAZ, 1024, "silu", ZT_na, not last),
                (O_RQ, 1024, "ropeq", QT_r, not last),
                (O_RK, 1024, "ropek", KT_r, True),
                (O_RV, 2048, "tok", V_r, True),
                (O_RZ, 2048, "silu", ZT_r, not last),
                (O_GNA, 4096, "sig", SG, not last),
            ]
            kscale = 128.0 ** -0.5
            ev = [0]
            for (c0, ncols, kind, dest, need_ctx) in specs:
                for cb in range(ncols // 512):
                    cc = c0 + cb * 512
                    if cc in pre_wt:
                        wt = pre_wt.pop(cc)
                    else:
                        wt = wt_pool.next()
                        k.dma("pool", wt, wt.ap[:], w_in,
                              w_in.ap[l, :, cc:cc + 512].rearrange("(k p) c -> p k c", p=128))
                    rope = kind in ("ropeq", "ropek")
                    if kind == "tok":
                        for t in range(NTT if need_ctx else 16):
                            pt = PS.next()
                            for kc in range(KC):
                                k.op("pe", lambda e, kc=kc, t=t: e.matmul(
                                    pt.ap[:], lhsT=hT.ap[:, kc, t * 128:(t + 1) * 128], rhs=wt.ap[:, kc, :],
                                    start=(kc == 0), stop=(kc == KC - 1)), [hTa, hTb, wt], [pt],
                                    signal=(kc == KC - 1))
                            st = stv_pool.next()
                            ev[0] += 1
                            if ev[0] % 2 or EV_ACT_ONLY:
                                k.op("act", lambda e: e.copy(out=st.ap[:], in_=pt.ap[:]), [pt], [st])
                            else:
                                k.op("dve", lambda e: e.tensor_copy(out=st.ap[:], in_=pt.ap[:]), [pt], [st])
                            k.dma("sp", dest, dest.ap[t * 128:(t + 1) * 128, cb * 512:(cb + 1) * 512],
                                  st, st.ap[:])
                        continue
                    blocks = lat_blocks + (ctx_block if need_ctx else [])
                    for j in range(4):
                        st = stg_pool.next()
                        for (t0, tn) in blocks:
                            is_ctx = t0 >= TL
                            pa = PS.next()
                            for kc in range(KC):
                                k.op("pe", lambda e, kc=kc: e.matmul(
                                    pa.ap[:, 0:tn], lhsT=wt.ap[:, kc, j * 128:(j + 1) * 128],
                                    rhs=hT.ap[:, kc, t0:t0 + tn], start=(kc == 0), stop=(kc == KC - 1)),
                                    [hTa, hTb, wt], [pa], signal=(kc == KC - 1))
                            dst = st.ap[:, t0:t0 + tn]
                            src = pa.ap[:, 0:tn]
                            if rope and not is_ctx:
                                pbf = pbf_pool.next()
                                k.op("act", lambda e, pbf=pbf, src=src, tn=tn: e.copy(out=pbf.ap[:, 0:tn], in_=src),
                                     [pa], [pbf])
                                sc_ = kscale if kind == "ropek" else 1.0

                                def fin(pa=pa, pbf=pbf, dst=dst, src=src, t0=t0, tn=tn, sc_=sc_, st=st):
                                    pb = PS.next()
                                    k.op("pe", lambda e: e.matmul(pb.ap[:, 0:tn], lhsT=permb.ap[:], rhs=pbf.ap[:, 0:tn],
                                                                  start=True, stop=True), [permb, pbf], [pb])
                                    t1 = t1p.next()
                                    t2 = t2p.next()
                                    k.op("dve", lambda e: e.scalar_tensor_tensor(
                                        out=t1.ap[:, 0:tn], in0=src, scalar=sc_, in1=ropeC.ap[:, t0:t0 + tn],
                                        op0=ALU.mult, op1=ALU.mult), [pa, ropeC, pbf], [t1])
                                    k.op("dve", lambda e: e.scalar_tensor_tensor(
                                        out=t2.ap[:, 0:tn], in0=pb.ap[:, 0:tn], scalar=sc_,
                                        in1=ropeS.ap[:, t0:t0 + tn], op0=ALU.mult, op1=ALU.mult),
                                        [pb, ropeS], [t2])
                                    k.op(POOL_ENG, lambda e: e.tensor_tensor(
                                        out=dst, in0=t1.ap[:, 0:tn], in1=t2.ap[:, 0:tn], op=ALU.add),
                                        [t1, t2], [st])
                                if deferred:
                                    deferred.pop()()
                                deferred.append(fin)
                            elif kind == "q":
                                k.op("act", lambda e: e.activation(out=dst, in_=src, func=AF.Copy,
                                                                   scale=kscale), [pa], [st])
                            elif kind == "ropek":
                                k.op("act", lambda e: e.activation(out=dst, in_=src, func=AF.Copy,
                                                                   scale=kscale), [pa], [st])
                            elif kind in ("copy", "ropeq"):
                                ev[0] += 1
                                if ev[0] % 2 or EV_ACT_ONLY:
                                    k.op("act", lambda e: e.copy(out=dst, in_=src), [pa], [st])
                                else:
                                    k.op("dve", lambda e: e.tensor_copy(out=dst, in_=src), [pa], [st])
                            elif kind == "silu":
                                k.op("act", lambda e: e.activation(out=dst, in_=src, func=AF.Silu), [pa], [st])
                            elif kind == "sig":
                                k.op("act", lambda e: e.activation(out=dst, in_=src, func=AF.Sigmoid), [pa], [st])
                        while deferred:
                            deferred.pop()()
                        tn_all = T if need_ctx else TL
                        r0 = cb * 512 + j * 128
                        k.dma("sp", dest, dest.ap[r0:r0 + 128, 0:tn_all], st, st.ap[:, 0:tn_all])
            k.barrier()
            k.release(ph)
        k.release(lay12)
        lay12.close()
        if stop_after == "2":
            break
        env = dict(nc=nc, k=k, sb=sb, reg=reg, PS=PS, m_tick=(m_tick if (defer_m and l == layers[0]) else None), l=l, last=last, rstd_ops=rstd_ops,
                   identb=identb, onesb=onesb, modT=modT)
        _phase_na(env, QT_na, KT_na, V_na, ZT_na, natab, AT)
        if defer_m and l == layers[0]:
            m_end(layers[1])
            k.barrier()
            k.release(mstack)
            mstack.close()
            PS.bufs = list(psb)
        if stop_after == "na":
            break
        pre3a = contextlib.ExitStack()
        wnap = Pool([sb(pre3a, "wna%d" % i, [128, 8, 512], BF16) for i in range(2)])
        wrep = Pool([sb(pre3a, "wre%d" % i, [128, 16, 512], BF16) for i in range(2)])
        pre_w = {}
        for cg in range(2):
            wna, wre = wnap.next(), wrep.next()
            k.dma("pool", wna, wna.ap[:], w_pna,
                  w_pna.ap[l, :, cg * 512:(cg + 1) * 512].rearrange("(k p) c -> p k c", p=128))
            k.dma("pool", wre, wre.ap[:], w_pret,
                  w_pret.ap[l, :, cg * 512:(cg + 1) * 512].rearrange("(k p) c -> p k c", p=128))
            pre_w[cg] = (wna, wre)
        env.update(wnap=wnap, wrep=wrep, pre_w=pre_w)
        _phase_ret(env, QT_r, KT_r, V_r, ZT_r, dlog, rpos_in, rcol_in, AT)
        if stop_after == "ret":
            break
        _phase_3a(env, AT, SG, w_pna, w_pret, MT)
        k.release(pre3a)
        pre3a.close()
        if stop_after == "3a":
            break
        _phase_3b(env, MT, w_out, GV, fg_in, xsrc, X1, yout)
        if stop_after == "3b":
            break

    if stop_after is not None:
        _finish(nc, k, yout, modT, dump)
    else:
        k.barrier()
        k.release(ph)


def _pipeline(stages, skews, n):
    for s in range(n + max(skews)):
        for fn, sk in zip(stages, skews):
            i = s - sk
            if 0 <= i < n:
                fn(i)


def _phase_na(env, QT_na, KT_na, V_na, ZT_na, natab, AT):
    k, sb, PS, l, last = env["k"], env["sb"], env["PS"], env["l"], env["last"]
    onesb = env["onesb"]
    nqb = 16 if last else 18
    tn_all = TL if last else T
    with contextlib.ExitStack() as ph:
        def mk(nm, shape, dt, n=2):
            return Pool([sb(ph, "%s%d" % (nm, i), shape, dt) for i in range(n)])
        qtp, ktp, ztp = mk("naq", [128, T], BF16), mk("nak", [128, T], BF16), mk("naz", [128, T], BF16)
        vtp = mk("nav", [128, NTT, 128], BF16)
        tabp = mk("natab", [128, NPAT, 128], F32)
        tabbp = mk("natabb", [128, NPAT, 128], BF16)
        identb = env["identb"]
        astp = mk("naa", [128, T], BF16)
        tmpp = mk("natmp", [128, 5 * 128], F32, 4)
        pexp = mk("napexp", [128, 7 * 128], BF16, 5)
        recp = mk("narec", [128, 128], F32, 3)
        ofp = mk("naof", [128, 128], F32, 3)
        hb = {}

        def load_head(h):
            qt, kt, zt, vt, tab, ast = (qtp.next(), ktp.next(), ztp.next(), vtp.next(),
                                        tabp.next(), astp.next())
            rs = slice(h * 128, (h + 1) * 128)
            k.dma("sp", kt, kt.ap[:], KT_na, KT_na.ap[rs, :])
            k.dma("sp", qt, qt.ap[:, 0:tn_all], QT_na, QT_na.ap[rs, 0:tn_all])
            k.dma("sp", zt, zt.ap[:, 0:tn_all], ZT_na, ZT_na.ap[rs, 0:tn_all])
            for g in range(3):
                k.dma("sp", vt, vt.ap[:, g * 6:(g + 1) * 6, :], V_na,
                      V_na.ap[g * 768:(g + 1) * 768, rs].rearrange("(t p) d -> p t d", p=128))
            k.dma("sp", tab, tab.ap[:], natab, natab.ap[l, h])
            tabb = tabbp.next()
            k.op("act", lambda e: e.copy(out=tabb.ap[:], in_=tab.ap[:]), [tab], [tabb])
            hb[h] = (qt, kt, zt, vt, tabb, ast)

        items = [(h, qb) for h in range(NH) for qb in range(nqb)]
        st = [dict() for _ in items]
        load_head(0)

        def tiles_of(qb):
            loc = [t for t in range(16) if (qb, t) in _PAT_OF] if qb < 16 else []
            return [16, 17] + loc, loc

        def s1(i):
            h, qb = items[i]
            if qb == 5 and h + 1 < NH:
                load_head(h + 1)
            if env.get("m_tick") is not None and i % 4 == 3 and i > 8:
                env["m_tick"](1)
            qt, kt, zt, vt, tab, ast = hb[h]
            qs = slice(qb * 128, (qb + 1) * 128)
            tiles, loc = tiles_of(qb)
            pA = PS.next()
            pB = PS.next() if len(loc) > 2 else None
            for j, t in enumerate(tiles):
                bank = pA if j < 4 else pB
                c = (j % 4) * 128
                local = j >= 2
                last_ = (j == 3 or j == len(tiles) - 1)
                k.op("pe", lambda e, bank=bank, c=c, t=t, local=local: e.matmul(
                    bank.ap[:, c:c + 128], lhsT=kt.ap[:, t * 128:(t + 1) * 128], rhs=qt.ap[:, qs],
                    start=True, stop=not local), [kt, qt], [bank],
                    signal=(last_ and not local))
                if local:
                    pat = _PAT_OF[(qb, t)]
                    k.op("pe", lambda e, bank=bank, c=c, pat=pat: e.matmul(
                        bank.ap[:, c:c + 128], lhsT=identb.ap[:], rhs=tab.ap[:, pat, :],
                        start=False, stop=True), [identb, tab], [bank], signal=last_)
            st[i].update(pA=pA, pB=pB)

        def s2(i):
            h, qb = items[i]
            qt, kt, zt, vt, tab, ast = hb[h]
            tiles, loc = tiles_of(qb)
            nl = len(loc)
            pA, pB = st[i]["pA"], st[i]["pB"]
            pe_ = pexp.next()
            na_ = min(4, 2 + nl) * 128
            k.op("act", lambda e: e.activation(out=pe_.ap[:, 0:na_], in_=pA.ap[:, 0:na_], func=AF.Exp),
                 [pA], [pe_])
            if nl > 2:
                nb_ = (nl - 2) * 128
                k.op("act", lambda e: e.activation(out=pe_.ap[:, 512:512 + nb_], in_=pB.ap[:, 0:nb_],
                                                   func=AF.Exp), [pB], [pe_])
            st[i]["pe"] = pe_

        def s3(i):
            h, qb = items[i]
            qt, kt, zt, vt, tab, ast = hb[h]
            tiles, loc = tiles_of(qb)
            pe_ = st[i]["pe"]
            pC = PS.next()
            nt = len(tiles)
            for j, t in enumerate(tiles):
                k.op("pe", lambda e, j=j, t=t: e.matmul(
                    pC.ap[:, 0:128], lhsT=vt.ap[:, t, :], rhs=pe_.ap[:, j * 128:(j + 1) * 128],
                    start=(j == 0), stop=(j == nt - 1)), [vt, pe_], [pC], signal=False)
            for j, t in enumerate(tiles):
                k.op("pe", lambda e, j=j: e.matmul(
                    pC.ap[:, 128:256], lhsT=onesb.ap[:], rhs=pe_.ap[:, j * 128:(j + 1) * 128],
                    start=(j == 0), stop=(j == nt - 1)), [onesb, pe_], [pC], signal=(j == nt - 1))
            st[i]["pC"] = pC

        def s4(i):
            h, qb = items[i]
            qt, kt, zt, vt, tab, ast = hb[h]
            qs = slice(qb * 128, (qb + 1) * 128)
            pC = st[i]["pC"]
            rec = recp.next()
            of = ofp.next()
            k.op("dve", lambda e: e.reciprocal(out=rec.ap[:], in_=pC.ap[:, 128:256]), [pC], [rec])
            k.op("dve", lambda e: e.tensor_tensor(out=of.ap[:], in0=pC.ap[:, 0:128], in1=rec.ap[:],
                                                  op=ALU.mult), [pC, rec], [of])
            k.op(POOL_ENG, lambda e: e.tensor_tensor(out=ast.ap[:, qs], in0=of.ap[:], in1=zt.ap[:, qs],
                                                   op=ALU.mult), [of, zt], [ast])
            if qb == nqb - 1:
                rs = slice(h * 128, (h + 1) * 128)
                k.dma("sp", AT, AT.ap[rs, 0:tn_all], ast, ast.ap[:, 0:tn_all])
            st[i].clear()

        _pipeline([s1, s2, s3, s4], NA_SKEW, len(items))
        if env.get("m_tick") is not None:
            env["m_tick"](64)
        k.barrier()
        k.release(ph)


def _phase_ret(env, QT_r, KT_r, V_r, ZT_r, dlog, rpos_in, rcol_in, AT):
    k, sb, PS, l, last = env["k"], env["sb"], env["PS"], env["l"], env["last"]
    identb, rstd_ops = env["identb"], env["rstd_ops"]
    tn_all = TL if last else T
    with contextlib.ExitStack() as ph:
        def mk(nm, shape, dt, n=2):
            return Pool([sb(ph, "%s%d" % (nm, i), shape, dt) for i in range(n)])
        rpos = sb(ph, "rpos", [128, 7, 128], F32)
        rcol = sb(ph, "rcol", [128, 2], F32)
        lg = sb(ph, "lg", [128, 16], F32)
        DT = sb(ph, "DT", [128, NH, 128], F32)
        DT2 = sb(ph, "DT2", [128, NH, 128], F32)
        qd = sb(ph, "qd", [128, NH, 2, 128], F32)
        kd = sb(ph, "kd", [128, NH, 2], F32)
        gd = sb(ph, "gd", [128, 16], F32)
        k.dma("sp", rpos, rpos.ap[:], rpos_in, rpos_in.ap)
        k.dma("sp", rcol, rcol.ap[:], rcol_in, rcol_in.ap)
        k.dma("sp", lg, lg.ap[:], dlog, dlog.ap[:, l * 16:(l + 1) * 16])
        k.op("act", lambda e: e.activation(out=lg.ap[:], in_=lg.ap[:], func=AF.Exp, scale=-1.0), [lg], [lg])
        k.op("act", lambda e: e.activation(out=lg.ap[:], in_=lg.ap[:], func=AF.Ln, bias=1.0), [lg], [lg])
        k.op("dve", lambda e: e.tensor_scalar(out=lg.ap[:], in0=lg.ap[:], scalar1=-1.0, scalar2=None,
                                              op0=ALU.mult), [lg], [lg])
        k.op("act", lambda e: e.activation(out=gd.ap[:], in_=lg.ap[:], func=AF.Exp, scale=128.0), [lg], [gd])
        for h in range(NH):
            lf, lb = lg.ap[:, h:h + 1], lg.ap[:, 8 + h:9 + h]
            k.op("act", lambda e: e.activation(out=DT.ap[:, h, :], in_=rpos.ap[:, 0, :], func=AF.Exp, scale=lf),
                 [rpos, lg], [DT])
            k.op("act", lambda e: e.activation(out=DT2.ap[:, h, :], in_=rpos.ap[:, 1, :], func=AF.Exp, scale=lb),
                 [rpos, lg], [DT2])
            k.op("act", lambda e: e.activation(out=qd.ap[:, h, 0, :], in_=rpos.ap[:, 5, :], func=AF.Exp, scale=lf),
                 [rpos, lg], [qd])
            k.op("act", lambda e: e.activation(out=qd.ap[:, h, 1, :], in_=rpos.ap[:, 6, :], func=AF.Exp, scale=lb),
                 [rpos, lg], [qd])
            k.op("act", lambda e: e.activation(out=kd.ap[:, h, 0:1], in_=rcol.ap[:, 0:1], func=AF.Exp, scale=lf),
                 [rcol, lg], [kd])
            k.op("act", lambda e: e.activation(out=kd.ap[:, h, 1:2], in_=rcol.ap[:, 1:2], func=AF.Exp, scale=lb),
                 [rcol, lg], [kd])
        bc = lambda a: a.unsqueeze(1).to_broadcast([128, NH, 128])
        k.op("dve", lambda e: e.tensor_tensor(out=DT.ap[:], in0=DT.ap[:], in1=bc(rpos.ap[:, 2, :]), op=ALU.mult),
             [DT, rpos], [DT])
        k.op("dve", lambda e: e.tensor_tensor(out=DT2.ap[:], in0=DT2.ap[:], in1=bc(rpos.ap[:, 3, :]), op=ALU.mult),
             [DT2, rpos], [DT2])
        k.op("dve", lambda e: e.tensor_tensor(out=DT.ap[:], in0=DT.ap[:], in1=DT2.ap[:], op=ALU.add),
             [DT, DT2], [DT])
        k.op("dve", lambda e: e.tensor_tensor(out=DT.ap[:], in0=DT.ap[:], in1=bc(rpos.ap[:, 4, :]), op=ALU.add),
             [DT, rpos], [DT])

        qtp, ktp = mk("rq", [128, T], BF16), mk("rk", [128, T], BF16)
        qfp, qbp = mk("rqf", [128, T], BF16, 1), mk("rqb", [128, T], BF16, 1)
        vp = mk("rv", [128, NTT, 256], BF16)
        zp = mk("rz", [128, 2, T], BF16)
        ap_ = mk("ra", [128, 2, T], BF16)
        kFp, kBp = mk("rkF", [128, NTT, 128], BF16, 1), mk("rkB", [128, NTT, 128], BF16, 1)
        Fbp, Bbp = mk("rFb", [128, NTT + 2, 256], BF16, 1), mk("rBb", [128, NTT + 2, 256], BF16, 1)
        Fm = mk("rFm", [128, 256], F32, 2)
        Bm = mk("rBm", [128, 256], F32, 2)
        sdp = mk("rsd", [128, 128], BF16, 5)
        onp = mk("ron", [128, 256], BF16, 6)
        jkp = mk("rjk", [128, 256], BF16, 3)
        ssp = mk("rss", [128, 2], F32, 6)

        hb = {}

        def load_head(h):
            qT, kT, v, z, a = qtp.next(), ktp.next(), vp.next(), zp.next(), ap_.next()
            rs = slice(h * 128, (h + 1) * 128)
            k.dma("sp", kT, kT.ap[:], KT_r, KT_r.ap[rs, :])
            k.dma("sp", qT, qT.ap[:, 0:tn_all], QT_r, QT_r.ap[rs, 0:tn_all])
            for g in range(3):
                k.dma("sp", v, v.ap[:, g * 6:(g + 1) * 6, :], V_r,
                      V_r.ap[g * 768:(g + 1) * 768, h * 256:(h + 1) * 256].rearrange("(t p) d -> p t d", p=128))
            for j in range(2):
                r0 = h * 256 + j * 128
                k.dma("sp", z, z.ap[:, j, 0:tn_all], ZT_r, ZT_r.ap[r0:r0 + 128, 0:tn_all])
            hb[h] = (qT, kT, v, z, a)

        load_head(0)
        for h in range(NH):
            qT, kT, v, z, a = hb[h]
            if h + 1 < NH:
                load_head(h + 1)
            qf, qb_ = qfp.next(), qbp.next()
            kF, kB, Fb, Bb = kFp.next(), kBp.next(), Fbp.next(), Bbp.next()
            ntq = tn_all // 128
            qv = lambda b: b.ap[:, 0:tn_all].rearrange("p (t c) -> p t c", c=128)
            k.op(POOL_ENG, lambda e: e.tensor_tensor(
                out=qv(qf), in0=qv(qT), in1=qd.ap[:, h, 0, :].unsqueeze(1).to_broadcast([128, ntq, 128]),
                op=ALU.mult), [qT, qd], [qf])
            k.op(POOL_ENG, lambda e: e.tensor_tensor(
                out=qv(qb_), in0=qv(qT), in1=qd.ap[:, h, 1, :].unsqueeze(1).to_broadcast([128, ntq, 128]),
                op=ALU.mult), [qT, qd], [qb_])
            for t in range(NTT):
                pk = PS.next()
                pkb = pk.ap[:].bitcast(BF16)
                k.op("pe", lambda e, t=t: e.transpose(out=pkb[:, 0:128], in_=kT.ap[:, t * 128:(t + 1) * 128],
                                                      identity=identb.ap[:]), [kT, identb], [pk])
                k.op("act", lambda e, t=t: e.activation(out=kF.ap[:, t, :], in_=pkb[:, 0:128], func=AF.Copy,
                                                        scale=kd.ap[:, h, 0:1]), [pk, kd], [kF])
                k.op("dve", lambda e, t=t: e.tensor_scalar(out=kB.ap[:, t, :], in0=pkb[:, 0:128],
                                                           scalar1=kd.ap[:, h, 1:2], scalar2=None,
                                                           op0=ALU.mult), [pk, kd], [kB])
            gF, gB = gd.ap[:, h:h + 1], gd.ap[:, 8 + h:9 + h]

            def run_seq(chunks, F0, Bend, need_out, slot0):
                n = len(chunks)
                Fcur = Fm.next()
                if F0 is None:
                    k.op("dve", lambda e: e.memset(Fcur.ap[:], 0.0), [], [Fcur])
                else:
                    k.op("dve", lambda e: e.tensor_copy(out=Fcur.ap[:], in_=F0.ap[:]), [F0], [Fcur])
                for i, t in enumerate(chunks):
                    k.op("act", lambda e, i=i: e.copy(out=Fb.ap[:, slot0 + i, :], in_=Fcur.ap[:]), [Fcur], [Fb])
                    pd = PS.next()
                    k.op("pe", lambda e, t=t: e.matmul(pd.ap[:, 0:256], lhsT=kF.ap[:, t, :], rhs=v.ap[:, t, :],
                                                       start=True, stop=True), [kF, v], [pd])
                    Fn = Fm.next()
                    k.op("dve", lambda e: e.scalar_tensor_tensor(
                        out=Fn.ap[:], in0=Fcur.ap[:], scalar=gF, in1=pd.ap[:, 0:256],
                        op0=ALU.mult, op1=ALU.add), [Fcur, pd, gd], [Fn])
                    Fcur = Fn
                Bcur = Bm.next()
                if Bend is None:
                    k.op("dve", lambda e: e.memset(Bcur.ap[:], 0.0), [], [Bcur])
                else:
                    k.op("dve", lambda e: e.tensor_copy(out=Bcur.ap[:], in_=Bend.ap[:]), [Bend], [Bcur])
                for i in range(n - 1, -1, -1):
                    t = chunks[i]
                    k.op("act", lambda e, i=i: e.copy(out=Bb.ap[:, slot0 + i, :], in_=Bcur.ap[:]), [Bcur], [Bb])
                    pd = PS.next()
                    k.op("pe", lambda e, t=t: e.matmul(pd.ap[:, 0:256], lhsT=kB.ap[:, t, :], rhs=v.ap[:, t, :],
                                                       start=True, stop=True), [kB, v], [pd])
                    Bn = Bm.next()
                    k.op("dve", lambda e: e.scalar_tensor_tensor(
                        out=Bn.ap[:], in0=Bcur.ap[:], scalar=gB, in1=pd.ap[:, 0:256],
                        op0=ALU.mult, op1=ALU.add), [Bcur, pd, gd], [Bn])
                    Bcur = Bn
                if need_out:
                    stt = [dict() for _ in chunks]

                    def r1(i):
                        t = chunks[i]
                        ts_ = slice(t * 128, (t + 1) * 128)
                        pS = PS.next()
                        k.op("pe", lambda e: e.matmul(pS.ap[:, 0:128], lhsT=kT.ap[:, ts_], rhs=qT.ap[:, ts_],
                                                      start=True, stop=True), [kT, qT], [pS])
                        sd = sdp.next()
                        k.op("dve", lambda e: e.tensor_tensor(out=sd.ap[:], in0=pS.ap[:, 0:128],
                                                              in1=DT.ap[:, h, :], op=ALU.mult), [pS, DT], [sd])
                        stt[i]["sd"] = sd

                    def r2(i):
                        t = chunks[i]
                        ts_ = slice(t * 128, (t + 1) * 128)
                        sd = stt[i]["sd"]
                        pO = PS.next()
                        k.op("pe", lambda e: e.matmul(pO.ap[:, 0:256], lhsT=sd.ap[:], rhs=v.ap[:, t, :],
                                                      start=True, stop=False), [sd, v], [pO], signal=False)
                        k.op("pe", lambda e: e.matmul(pO.ap[:, 0:256], lhsT=qf.ap[:, ts_], rhs=Fb.ap[:, slot0 + i, :],
                                                      start=False, stop=False), [qf, Fb], [pO], signal=False)
                        k.op("pe", lambda e: e.matmul(pO.ap[:, 0:256], lhsT=qb_.ap[:, ts_], rhs=Bb.ap[:, slot0 + i, :],
                                                      start=False, stop=True), [qb_, Bb], [pO])
                        ss, jk, on = ssp.next(), jkp.next(), onp.next()
                        k.op("act", lambda e: e.activation(out=jk.ap[:], in_=pO.ap[:, 0:256], func=AF.Square,
                                                           accum_out=ss.ap[:, 0:1]), [pO], [jk, ss])
                        rstd_ops(ss, 1.0 / 256)
                        stt[i].update(on=on, pO=pO, ss=ss)

                    def r2b(i):
                        on, pO, ss = stt[i]["on"], stt[i]["pO"], stt[i]["ss"]
                        k.op("act", lambda e: e.activation(out=on.ap[:], in_=pO.ap[:, 0:256], func=AF.Copy,
                                                           scale=ss.ap[:, 1:2]), [pO, ss], [on])

                    def r3(i):
                        t = chunks[i]
                        ts_ = slice(t * 128, (t + 1) * 128)
                        on = stt[i]["on"]
                        pT = PS.next()
                        pTb = pT.ap[:].bitcast(BF16)
                        for j in range(2):
                            k.op("pe", lambda e, j=j: e.transpose(
                                out=pTb[:, j * 128:(j + 1) * 128], in_=on.ap[:, j * 128:(j + 1) * 128],
                                identity=identb.ap[:]), [on, identb], [pT], signal=(j == 1))
                        k.op("dve", lambda e: e.tensor_tensor(
                            out=a.ap[:, :, ts_], in0=pTb[:, 0:256].rearrange("p (j c) -> p j c", j=2),
                            in1=z.ap[:, :, ts_], op=ALU.mult), [pT, z], [a])
                        stt[i].clear()

                    _pipeline([r1, r2, r2b, r3], [0, 2, 3, 5], len(chunks))
                return Fcur, Bcur

            Fc, Bc = run_seq([16, 17], None, None, not last, 0)
            run_seq(list(range(16)), Fc, Bc, True, 2)
            for j in range(2):
                r0 = 1024 + h * 256 + j * 128
                k.dma("sp", AT, AT.ap[r0:r0 + 128, 0:tn_all], a, a.ap[:, j, 0:tn_all])
        k.barrier()
        k.release(ph)


def _phase_3a(env, AT, SG, w_pna, w_pret, MT):
    k, sb, PS, l, last = env["k"], env["sb"], env["PS"], env["l"], env["last"]
    tn_all = TL if last else T
    blocks = [(i * 512, 512) for i in range(4)] + ([] if last else [(TL, TC)])
    with contextlib.ExitStack() as ph:
        def mk(nm, shape, dt, n=2):
            return Pool([sb(ph, "%s%d" % (nm, i), shape, dt) for i in range(n)])
        a_all = sb(ph, "a_all", [128, 24, tn_all], BF16, coarse=True)
        a_blk = [env["reg"](Buf(a_all.ap, "a_blk%d" % i, coarse=True)) for i in range(len(blocks))]
        ph._bufs.extend(a_blk)

        def load_ablk(bi):
            t0, tn = blocks[bi]
            for c0 in (0, 8, 16):
                k.dma("sp", a_blk[bi], a_all.ap[:, c0:c0 + 8, t0:t0 + tn], AT,
                      AT.ap[c0 * 128:(c0 + 8) * 128, t0:t0 + tn].rearrange("(c p) t -> p c t", p=128),
                      sem_buf=a_blk[bi])
        load_ablk(0)
        wnap, wrep, pre_w = env["wnap"], env["wrep"], env["pre_w"]
        sgnp, sgrp = mk("sgn", [128, tn_all], BF16), mk("sgr", [128, tn_all], BF16)
        mstp = mk("mst", [128, tn_all], BF16)
        t1p, t2p = mk("m1", [128, 512], F32), mk("m2", [128, 512], F32)

        def load_sg(dc):
            sgn, sgr = sgnp.next(), sgrp.next()
            k.dma("sp", sgn, sgn.ap[:], SG, SG.ap[dc * 128:(dc + 1) * 128, 0:tn_all])
            k.dma("sp", sgr, sgr.ap[:], SG, SG.ap[2048 + dc * 128:2048 + (dc + 1) * 128, 0:tn_all])
            return sgn, sgr
        for cg in range(4):
            if cg in pre_w:
                wna, wre = pre_w[cg]
            else:
                wna, wre = wnap.next(), wrep.next()
                k.dma("pool", wna, wna.ap[:], w_pna,
                      w_pna.ap[l, :, cg * 512:(cg + 1) * 512].rearrange("(k p) c -> p k c", p=128))
                k.dma("pool", wre, wre.ap[:], w_pret,
                      w_pret.ap[l, :, cg * 512:(cg + 1) * 512].rearrange("(k p) c -> p k c", p=128))
            for j in range(4):
                dc = cg * 4 + j
                if dc == 0:
                    sg_next = load_sg(0)
                    for bi in range(1, len(blocks)):
                        load_ablk(bi)
                sgn, sgr = sg_next
                mst = mstp.next()
                if dc + 1 < 16:
                    sg_next = load_sg(dc + 1)
                for bi, (t0, tn) in enumerate(blocks):
                    ab = a_blk[bi]
                    pn, pr = PS.next(), PS.next()
                    for kc in range(8):
                        k.op("pe", lambda e, kc=kc: e.matmul(
                            pn.ap[:, 0:tn], lhsT=wna.ap[:, kc, j * 128:(j + 1) * 128],
                            rhs=a_all.ap[:, kc, t0:t0 + tn], start=(kc == 0), stop=(kc == 7)),
                            [wna, ab], [pn], signal=(kc == 7))
                    for kc in range(16):
                        k.op("pe", lambda e, kc=kc: e.matmul(
                            pr.ap[:, 0:tn], lhsT=wre.ap[:, kc, j * 128:(j + 1) * 128],
                            rhs=a_all.ap[:, 8 + kc, t0:t0 + tn], start=(kc == 0), stop=(kc == 15)),
                            [wre, ab], [pr], signal=(kc == 15))
                    t1, t2 = t1p.next(), t2p.next()
                    k.op("dve", lambda e: e.tensor_tensor(out=t1.ap[:, 0:tn], in0=pn.ap[:, 0:tn],
                                                          in1=sgn.ap[:, t0:t0 + tn], op=ALU.mult), [pn, sgn], [t1])
                    k.op("dve", lambda e: e.tensor_tensor(out=t2.ap[:, 0:tn], in0=pr.ap[:, 0:tn],
                                                          in1=sgr.ap[:, t0:t0 + tn], op=ALU.mult), [pr, sgr], [t2])
                    k.op(POOL_ENG, lambda e: e.tensor_tensor(out=mst.ap[:, t0:t0 + tn], in0=t1.ap[:, 0:tn],
                                                           in1=t2.ap[:, 0:tn], op=ALU.add), [t1, t2], [mst])
                k.dma("sp", MT, MT.ap[dc * 128:(dc + 1) * 128, 0:tn_all], mst, mst.ap[:])
        k.barrier()
        k.release(ph)


def _phase_3b(env, MT, w_out, GV, fg_in, xsrc, X1, yout):
    k, sb, PS, l, last = env["k"], env["sb"], env["PS"], env["l"], env["last"]
    rstd_ops = env["rstd_ops"]
    tn_all = TL if last else T
    with contextlib.ExitStack() as ph:
        def mk(nm, shape, dt, n=2):
            return Pool([sb(ph, "%s%d" % (nm, i), shape, dt) for i in range(n)])
        mt_all = sb(ph, "mt_all", [128, KC, tn_all], BF16, coarse=True)
        wo = sb(ph, "wo", [128, KC, D], BF16, coarse=True)
        nr = 1 if last else 2
        gate = sb(ph, "gate", [128, nr, D], F32, coarse=True)
        ngrp = (tn_all + 767) // 768
        mt_grp = [env["reg"](Buf(mt_all.ap, "mt_grp%d" % i, coarse=True)) for i in range(ngrp)]
        wo_cg = [env["reg"](Buf(wo.ap, "wo_cg%d" % i, coarse=True)) for i in range(4)]
        ph._bufs.extend(mt_grp + wo_cg)
        for cg in range(4):
            k.dma("pool", wo_cg[cg], wo.ap[:, :, cg * 512:(cg + 1) * 512], w_out,
                  w_out.ap[l, :, cg * 512:(cg + 1) * 512].rearrange("(k p) c -> p k c", p=128), sem_buf=wo_cg[cg])
        for g in range(ngrp):
            g0 = g * 768
            gn = min(768, tn_all - g0)
            for c0 in (0, 8):
                k.dma("sp", mt_grp[g], mt_all.ap[:, c0:c0 + 8, g0:g0 + gn], MT,
                      MT.ap[c0 * 128:(c0 + 8) * 128, g0:g0 + gn].rearrange("(c p) t -> p c t", p=128),
                      sem_buf=mt_grp[g])
        for r in range(nr):
            k.dma("sp", gate, gate.ap[:, r, :], GV, GV.ap[l * 2 + r:l * 2 + r + 1, :].partition_broadcast(128)
                  if False else GV.ap[l * 2 + r].partition_broadcast(128), sem_buf=gate)
        if last:
            fg = sb(ph, "fg", [128, D], F32)
            k.dma("sp", fg, fg.ap[:], fg_in, fg_in.ap[0].partition_broadcast(128))
            jk = sb(ph, "jk3", [128, D], BF16)
        xtp = mk("x3", [128, D], F32, 3)
        tmpp = mk("tmp3", [128, 512], F32, 2)
        ssp = mk("ss3", [128, 2], F32, 2)
        nt3 = tn_all // 128

        def load_x(t):
            xt = xtp.next()
            k.dma("sp", xt, xt.ap[:], xsrc, xsrc.ap[t * 128:(t + 1) * 128, :])
            return xt
        x_next = load_x(0)
        for t in range(nt3):
            r = 0 if t < 16 else 1
            xt = x_next
            if t + 1 < nt3:
                x_next = load_x(t + 1)
            for cg in range(4):
                cs = slice(cg * 512, (cg + 1) * 512)
                po = PS.next()
                for kc in range(KC):
                    k.op("pe", lambda e, kc=kc: e.matmul(
                        po.ap[:], lhsT=mt_all.ap[:, kc, t * 128:(t + 1) * 128], rhs=wo.ap[:, kc, cs],
                        start=(kc == 0), stop=(kc == KC - 1)), [mt_grp[(t * 128) // 768], wo_cg[cg]], [po],
                        signal=(kc == KC - 1))
                tmp = tmpp.next()
                k.op("dve", lambda e: e.tensor_tensor(out=tmp.ap[:], in0=po.ap[:], in1=gate.ap[:, r, cs],
                                                      op=ALU.mult), [po, gate], [tmp])
                k.op(POOL_ENG, lambda e: e.tensor_tensor(out=xt.ap[:, cs], in0=xt.ap[:, cs], in1=tmp.ap[:],
                                                       op=ALU.add), [xt, tmp], [xt])
            if last:
                ss = ssp.next()
                k.op("act", lambda e: e.activation(out=jk.ap[:], in_=xt.ap[:], func=AF.Square,
                                                   accum_out=ss.ap[:, 0:1]), [xt], [jk, ss])
                rstd_ops(ss, 1.0 / D)
                k.op("dve", lambda e: e.scalar_tensor_tensor(
                    out=xt.ap[:], in0=xt.ap[:], scalar=ss.ap[:, 1:2], in1=fg.ap[:],
                    op0=ALU.mult, op1=ALU.mult), [xt, ss, fg], [xt])
                k.dma("sp", yout, yout.ap[t * 128:(t + 1) * 128, :], xt, xt.ap[:])
            else:
                k.dma("sp", X1, X1.ap[t * 128:(t + 1) * 128, :], xt, xt.ap[:])
        k.barrier()
        k.release(ph)


def _finish(nc, k, yout, modT, dump):
    k.barrier()
    k.dma("sp", yout, yout.ap[0:128, 0:192], modT, modT.ap[:].rearrange("p l g r -> p (l g r)"),
          sem_buf=modT)
    k.barrier()


def _host_inputs(x, c, ctx, c_ctx, ada_w, ada_b, norm_g, w_in, na_rpb, ret_decay_logit,
                 w_proj_na, w_proj_ret, w_out, final_g):
    f = np.float32
    shared = {}
    shared["ada_w"] = np.ascontiguousarray(ada_w, f)
    shared["adabT"] = np.ascontiguousarray(
        np.asarray(ada_b, f).reshape(DEPTH, 48, 128).transpose(2, 0, 1).reshape(128, DEPTH * 48))
    shared["gT"] = np.ascontiguousarray(
        np.asarray(norm_g, f).reshape(DEPTH, KC, 128).transpose(2, 0, 1).reshape(128, DEPTH * KC))
    shared["w_in"] = np.ascontiguousarray(w_in, f)
    rpb = np.asarray(na_rpb, f)
    tab = rpb[:, :, _PDR, _PDC]
    tab = np.where(_PV[None, None], tab, f(NEG)).astype(f)
    shared["natab"] = np.ascontiguousarray(tab.transpose(0, 1, 3, 2, 4))
    dl = np.asarray(ret_decay_logit, f).reshape(1, DEPTH * 16)
    shared["dlog"] = np.ascontiguousarray(np.broadcast_to(dl, (128, DEPTH * 16)))
    shared["w_pna"] = np.ascontiguousarray(w_proj_na, f)
    shared["w_pret"] = np.ascontiguousarray(w_proj_ret, f)
    shared["w_out"] = np.ascontiguousarray(w_out, f)
    shared["final_g"] = np.ascontiguousarray(np.asarray(final_g, f).reshape(1, D))
    shared["ident"] = np.eye(128, dtype=f)
    shared["perm"] = np.ascontiguousarray(np.roll(np.eye(128, dtype=f), 64, axis=0))
    C, S = _rope_tables()
    shared["ropeC"] = C
    shared["ropeS"] = S
    tabs, cols = _ret_pos_tables()
    shared["rpos"] = tabs
    shared["rcol"] = cols
    maps = []
    cc = np.asarray(c_ctx, f).reshape(KC, 128).T
    for b in range(x.shape[0]):
        m = dict(shared)
        m["xin"] = np.ascontiguousarray(np.concatenate([np.asarray(x[b], f), np.asarray(ctx[b], f)], axis=0))
        cb = np.asarray(c[b], f).reshape(KC, 128).T
        m["cT"] = np.ascontiguousarray(np.concatenate([cb, cc], axis=1))
        maps.append(m)
    return maps


def kernel(**inputs):
    maps = _host_inputs(**inputs)
    nc = build_program()
    res = run_bass_kernel_spmd(nc, maps, core_ids=list(range(8)))
    return np.stack([r["yout"] for r in res.results], axis=0).astype(np.float32)
```

```python
import contextlib
import numpy as np
import concourse.bass as bass
import concourse.mybir as mybir
from concourse.bass_utils import run_bass_kernel_spmd

F32 = mybir.dt.float32
BF16 = mybir.dt.bfloat16
AF = mybir.ActivationFunctionType
ALU = mybir.AluOpType

D = 2048
KC = 16
TL = 2048
TC = 256
T = TL + TC
NTT = T // 128
IN_COLS = 14336
NH = 8
EPS = 1e-6
NEG = -30000.0
DEPTH = 2
EV_ACT_ONLY = False
POOL_ENG = 'dve'
RET_SKEW = [0, 2, 4]
NA_SKEW = [0, 0, 3, 3]

O_NAQ, O_NAK, O_NAV, O_NAZ = 0, 1024, 2048, 3072
O_RQ, O_RK, O_RV, O_RZ = 4096, 5120, 6144, 8192
O_GNA, O_GRET = 10240, 12288


class Buf:
    def __init__(self, ap, name, coarse=False):
        self.ap = ap
        self.name = name
        self.coarse = coarse
        self.writers = {}
        self.readers = {}
        self.dsem = None
        self.dcnt = 0


class KB:
    def __init__(self, nc, es):
        self.nc = nc
        self.es = es
        self.eng = {"pe": nc.tensor, "act": nc.scalar, "dve": nc.vector,
                    "pool": nc.gpsimd, "sp": nc.sync}
        self.sems = {}
        self.cnt = {}
        for n in ("pe", "act", "dve", "pool"):
            self.sems[n] = es.enter_context(nc.semaphore("s_" + n))
            self.cnt[n] = 0
        self.waited = {n: {} for n in self.eng}
        self.pe_pending = []
        self.nd = 0
        self.free_dsems = {"sw": [], "hw": []}

    def sbuf(self, es, name, shape, dtype, coarse=False):
        self.nalloc = getattr(self, "nalloc", 0) + 1
        name = "sb%d_%s" % (self.nalloc, name)
        t = es.enter_context(self.nc.sbuf_tensor(name, list(shape), dtype))
        return Buf(t, name, coarse)

    def psum(self, es, name, shape, dtype):
        t = es.enter_context(self.nc.psum_tensor(name, list(shape), dtype))
        return Buf(t, name)

    def dram(self, name, shape, dtype, kind="Internal"):
        t = self.nc.dram_tensor(name, list(shape), dtype, kind=kind).ap()
        return Buf(t, name, coarse=True)

    def _wait(self, e, toks):
        w = self.waited[e]
        for key, val in toks.items():
            if w.get(key, 0) < val:
                self.eng[e].wait_ge(self.sems[key], val)
                w[key] = val

    @staticmethod
    def _merge(dst, src):
        for k, v in src.items():
            if dst.get(k, 0) < v:
                dst[k] = v

    def _deps(self, reads, writes):
        d = {}
        for b in reads:
            self._merge(d, b.writers)
        for b in writes:
            self._merge(d, b.writers)
            self._merge(d, b.readers)
        return d

    def _mark(self, reads, writes, key, val):
        for b in reads:
            if b.readers.get(key, 0) < val:
                b.readers[key] = val
        for b in writes:
            if b.coarse:
                if b.writers.get(key, 0) < val:
                    b.writers[key] = val
            else:
                b.writers = {key: val}
                b.readers = {}

    def op(self, e, fn, reads=(), writes=(), signal=True):
        self._wait(e, self._deps(reads, writes))
        ins = fn(self.eng[e])
        if e == "pe" and not signal:
            self.pe_pending.append((list(reads), list(writes)))
            return ins
        self.cnt[e] += 1
        ins.then_inc(self.sems[e], 1)
        self._mark(reads, writes, e, self.cnt[e])
        if e == "pe":
            for r, w in self.pe_pending:
                self._mark(r, w, e, self.cnt[e])
            self.pe_pending = []
        return ins

    def dma(self, q, out_buf, out_ap, in_buf, in_ap, sem_buf=None, **kw):
        sb = sem_buf
        if sb is None:
            sb = out_buf if not out_buf.coarse else in_buf
        cls = "sw" if q == "pool" else "hw"
        if sb.dsem is None:
            sb.dsem = {}
        if cls not in sb.dsem:
            if self.free_dsems[cls]:
                sb.dsem[cls] = list(self.free_dsems[cls].pop())
            else:
                self.nd += 1
                key = "d%d" % self.nd
                self.sems[key] = self.es.enter_context(self.nc.semaphore(key))
                sb.dsem[cls] = [key, 0]
        ent = sb.dsem[cls]
        self._wait(q, self._deps([in_buf], [out_buf]))
        ins = self.eng[q].dma_start(out=out_ap, in_=in_ap, **kw)
        ent[1] += 16
        ins.then_inc(self.sems[ent[0]], 16)
        self._mark([in_buf], [out_buf], ent[0], ent[1])
        return ins

    def release(self, stack):
        for b in getattr(stack, "_bufs", []):
            if b.dsem:
                for cls, ent in b.dsem.items():
                    self.free_dsems[cls].append(tuple(ent))
                b.dsem = None

    def barrier(self):
        allt = {n: self.cnt[n] for n in ("pe", "act", "dve", "pool") if self.cnt[n] > 0}
        assert not self.pe_pending
        for b in self._all_bufs:
            if b.dsem:
                for ent in b.dsem.values():
                    allt[ent[0]] = ent[1]
        for cls in self.free_dsems:
            for key, val in self.free_dsems[cls]:
                allt[key] = val
        for e in self.eng:
            self._wait(e, allt)


class Pool:
    def __init__(self, bufs):
        self.bufs = bufs
        self.i = 0

    def next(self):
        b = self.bufs[self.i % len(self.bufs)]
        self.i += 1
        return b


def _na_patterns():
    rows, W = 32, 64
    kh, kw = 8, 16
    pats = {}
    pat_of = {}
    drs, dcs, vs = [], [], []
    kk = np.arange(128)
    krow_l, kcol = kk // 64, kk % 64
    for qb in range(16):
        qrow = 2 * qb + krow_l
        qcol = kcol
        r0 = np.clip(qrow - kh // 2, 0, rows - kh)
        c0 = np.clip(qcol - kw // 2, 0, W - kw)
        for kt in range(16):
            krow = 2 * kt + krow_l
            rv = (krow[:, None] >= r0[None, :]) & (krow[:, None] < r0[None, :] + kh)
            cv = (kcol[:, None] >= c0[None, :]) & (kcol[:, None] < c0[None, :] + kw)
            valid = rv & cv
            if not valid.any():
                continue
            dr = np.clip(krow[:, None] - qrow[None, :] + 7, 0, 14)
            dc = np.clip(kcol[:, None] - qcol[None, :] + 15, 0, 30)
            dr = np.where(valid, dr, 0).astype(np.int32)
            dc = np.where(valid, dc, 0).astype(np.int32)
            key = (dr.tobytes(), dc.tobytes(), valid.tobytes())
            if key not in pats:
                pats[key] = len(drs)
                drs.append(dr); dcs.append(dc); vs.append(valid)
            pat_of[(qb, kt)] = pats[key]
    return pat_of, np.stack(drs), np.stack(dcs), np.stack(vs)


_PAT_OF, _PDR, _PDC, _PV = _na_patterns()
NPAT = _PDR.shape[0]


def _rope_tables():
    t = np.arange(TL)
    row = (t // 64).astype(np.float32)
    col = (t % 64).astype(np.float32)
    nf = 32
    inv = (np.float32(10000.0) ** (-np.arange(nf, dtype=np.float32) / np.float32(nf))).astype(np.float32)
    ang = np.concatenate([row[:, None] * inv, col[:, None] * inv], axis=-1).astype(np.float32)
    cos = np.cos(ang).astype(np.float32).T
    sin = np.sin(ang).astype(np.float32).T
    C = np.concatenate([cos, cos], axis=0)
    S = np.concatenate([-sin, sin], axis=0)
    return np.ascontiguousarray(C), np.ascontiguousarray(S)


def _ret_pos_tables():
    p = np.arange(128, dtype=np.float32)
    key = p[:, None]
    qry = p[None, :]
    tabs = np.zeros((128, 7, 128), np.float32)
    tabs[:, 0, :] = np.maximum(qry - key, 0)
    tabs[:, 1, :] = np.maximum(key - qry, 0)
    tabs[:, 2, :] = (key < qry).astype(np.float32)
    tabs[:, 3, :] = (key > qry).astype(np.float32)
    tabs[:, 4, :] = 2.0 * (key == qry)
    tabs[:, 5, :] = np.broadcast_to(qry + 1.0, (128, 128))
    tabs[:, 6, :] = np.broadcast_to(128.0 - qry, (128, 128))
    cols = np.zeros((128, 2), np.float32)
    cols[:, 0] = 127.0 - p
    cols[:, 1] = p
    return tabs, cols


def build_program(dump=(), layers=(0, 1), stop_after=None):
    nc = bass.Bass("TRN2", target_bir_lowering=False)
    es = contextlib.ExitStack()
    with es:
        k = KB(nc, es)
        k._all_bufs = []
        _build(nc, es, k, dump, layers, stop_after)
    return nc


def _build(nc, es, k, dump, layers, stop_after):
    allb = k._all_bufs

    def reg(b):
        allb.append(b)
        return b

    def din(name, shape, dtype=F32):
        return reg(k.dram(name, shape, dtype, kind="ExternalInput"))

    def dscr(name, shape, dtype):
        kind = "ExternalOutput" if name in dump else "Internal"
        return reg(k.dram(name, shape, dtype, kind=kind))

    def sb(stack, name, shape, dtype, coarse=False):
        b = reg(k.sbuf(stack, name, shape, dtype, coarse))
        if not hasattr(stack, "_bufs"):
            stack._bufs = []
        stack._bufs.append(b)
        return b

    xin = din("xin", [T, D])
    cT = din("cT", [128, 32])
    ada_w = din("ada_w", [DEPTH, D, 3 * D])
    adabT = din("adabT", [128, DEPTH * 48])
    gT_in = din("gT", [128, DEPTH * KC])
    w_in = din("w_in", [DEPTH, D, IN_COLS])
    natab = din("natab", [DEPTH, NH, 128, NPAT, 128])
    dlog = din("dlog", [128, DEPTH * 16])
    w_pna = din("w_pna", [DEPTH, 1024, D])
    w_pret = din("w_pret", [DEPTH, 2048, D])
    w_out = din("w_out", [DEPTH, D, D])
    fg_in = din("final_g", [1, D])
    ident_in = din("ident", [128, 128])
    perm_in = din("perm", [128, 128])
    ropeC_in = din("ropeC", [128, TL])
    ropeS_in = din("ropeS", [128, TL])
    rpos_in = din("rpos", [128, 7, 128])
    rcol_in = din("rcol", [128, 2])
    yout = reg(k.dram("yout", [TL, D], F32, kind="ExternalOutput"))

    QT_na = dscr("QT_na", [1024, T], BF16)
    KT_na = dscr("KT_na", [1024, T], BF16)
    V_na = dscr("V_na", [T, 1024], BF16)
    ZT_na = dscr("ZT_na", [1024, T], BF16)
    QT_r = dscr("QT_r", [1024, T], BF16)
    KT_r = dscr("KT_r", [1024, T], BF16)
    V_r = dscr("V_r", [T, 2048], BF16)
    ZT_r = dscr("ZT_r", [2048, T], BF16)
    SG = dscr("SG", [4096, T], BF16)
    AT = dscr("AT", [3072, T], BF16)
    MT = dscr("MT", [2048, T], BF16)
    X1 = dscr("X1", [T, D], F32)
    GV = dscr("GV", [DEPTH * 2, D], F32)
    HTd = dscr("HTd", [D, T], BF16) if "HTd" in dump else None

    ident = sb(es, "ident", [128, 128], F32)
    identb = sb(es, "identb", [128, 128], BF16)
    onesb = sb(es, "onesb", [128, 128], BF16)
    modT = sb(es, "modT", [128, DEPTH, 48, 2], F32)
    gTt = sb(es, "gTt", [128, DEPTH * KC], F32)
    psb = [reg(k.psum(es, "ps%d" % i, [128, 512], F32)) for i in range(8)]
    PS = Pool(psb[:7])
    psM = psb[7]
    epsc = sb(es, "epsc", [128, 1], F32)
    k.op("dve", lambda e: e.memset(epsc.ap[:], EPS), [], [epsc])

    def rstd_ops(ss, inv_n):
        k.op("act", lambda e: e.activation(out=ss.ap[:, 1:2], in_=ss.ap[:, 0:1], func=AF.Sqrt,
                                           scale=inv_n, bias=epsc.ap[:, 0:1]), [ss, epsc], [ss])
        k.op("dve", lambda e: e.reciprocal(out=ss.ap[:, 1:2], in_=ss.ap[:, 1:2]), [ss], [ss])

    k.dma("sp", ident, ident.ap[:], ident_in, ident_in.ap)
    permf = sb(es, "permf", [128, 128], F32)
    permb = sb(es, "permb", [128, 128], BF16)
    k.dma("sp", permf, permf.ap[:], perm_in, perm_in.ap)
    k.op("dve", lambda e: e.tensor_copy(out=permb.ap[:], in_=permf.ap[:]), [permf], [permb])
    k.dma("sp", gTt, gTt.ap[:], gT_in, gT_in.ap)
    k.op("dve", lambda e: e.tensor_copy(out=identb.ap[:], in_=ident.ap[:]), [ident], [identb])
    k.op("dve", lambda e: e.memset(onesb.ap[:], 1.0), [], [onesb])

    mstack = contextlib.ExitStack()
    HALF = 3 * D // 2
    sc = sb(mstack, "sc", [128, 32], F32)
    adab = sb(mstack, "adab", [128, DEPTH * 48], F32)
    wa = Pool([sb(mstack, "wa%d" % i, [128, HALF], F32) for i in range(2)])
    wb = Pool([sb(mstack, "wb%d" % i, [128, HALF], BF16) for i in range(2)])
    scb = sb(mstack, "scb", [128, 32], BF16)
    k.dma("sp", sc, sc.ap[:], cT, cT.ap)
    k.dma("sp", adab, adab.ap[:], adabT, adabT.ap)
    k.op("act", lambda e: e.activation(out=sc.ap[:], in_=sc.ap[:], func=AF.Silu), [sc], [sc])
    k.op("dve", lambda e: e.tensor_copy(out=scb.ap[:], in_=sc.ap[:]), [sc], [scb])
    sc3 = scb.ap[:].rearrange("p (r k) -> p k r", r=2)
    mst8 = {}
    NM = 2 * KC

    def m_begin(l):
        k.op("dve", lambda e: e.memset(psM.ap[:], 0.0), [], [psM])

    def m_load(l, i):
        kc, hf = divmod(i, 2)
        w = wa.next()
        k.dma("sp", w, w.ap[:], ada_w, ada_w.ap[l, kc * 128:(kc + 1) * 128, hf * HALF:(hf + 1) * HALF])
        mst8[(l, i)] = w

    def m_compute(l, i, cast_eng):
        kc, hf = divmod(i, 2)
        w = mst8.pop((l, i))
        wbf = wb.next()
        if cast_eng == "act":
            k.op("act", lambda e: e.copy(out=wbf.ap[:], in_=w.ap[:]), [w], [wbf])
        else:
            k.op("dve", lambda e: e.tensor_copy(out=wbf.ap[:], in_=w.ap[:]), [w], [wbf])
        for g in range(24):
            gg = hf * 24 + g
            k.op("pe", lambda e, g=g, gg=gg: e.matmul(
                psM.ap[:, 2 * gg:2 * gg + 2], lhsT=wbf.ap[:, g * 128:(g + 1) * 128],
                rhs=sc3[:, kc, :], start=False, stop=(kc == KC - 1), skip_group_check=True),
                [wbf, scb], [psM], signal=(g == 23))

    def m_end(l):
        bias_bc = adab.ap[:, l * 48:(l + 1) * 48].unsqueeze(2).to_broadcast([128, 48, 2])
        k.op("dve", lambda e: e.tensor_tensor(
            out=modT.ap[:, l], in0=psM.ap[:, 0:96].rearrange("p (g r) -> p g r", r=2),
            in1=bias_bc, op=ALU.add), [psM, adab], [modT])
        for r in range(2):
            with nc.allow_non_contiguous_dma(reason="tiny gate vector transpose"):
                k.dma("sp", GV, GV.ap[l * 2 + r].rearrange("(k p) -> p k", p=128),
                      modT, modT.ap[:, l, 32:48, r], sem_buf=modT)

    defer_m = (len(layers) == 2 and stop_after is None)
    for l in (layers[:1] if defer_m else layers):
        m_begin(l)
        m_load(l, 0)
        for i in range(NM):
            if i + 1 < NM:
                m_load(l, i + 1)
            m_compute(l, i, "act" if i % 2 == 0 else "dve")
        m_end(l)
    k.barrier()
    if not defer_m:
        k.release(mstack)
        mstack.close()
        PS.bufs = list(psb)
    mi = [0]

    def m_tick(n):
        for _ in range(n):
            i = mi[0]
            if i >= NM:
                return
            if i == 0:
                m_begin(layers[1])
                m_load(layers[1], 0)
            if i + 1 < NM:
                m_load(layers[1], i + 1)
            m_compute(layers[1], i, "act")
            mi[0] += 1

    if stop_after == "M":
        _finish(nc, k, yout, modT, dump)
        return

    for l in layers:
        lay12 = contextlib.ExitStack()
        hT = sb(lay12, "hT", [128, KC, T], BF16, coarse=True)
        hTa = reg(Buf(hT.ap, "hTa", coarse=True))
        hTb = reg(Buf(hT.ap, "hTb", coarse=True))
        wt_pool = Pool([sb(lay12, "wt%d" % i, [128, KC, 512], BF16) for i in range(2)])
        pre_wt = {}
        for cc in (0, 512):
            wt = wt_pool.next()
            k.dma("pool", wt, wt.ap[:], w_in,
                  w_in.ap[l, :, cc:cc + 512].rearrange("(k p) c -> p k c", p=128))
            pre_wt[cc] = wt
        last = (l == DEPTH - 1)
        ntt = NTT if True else 16
        xsrc = xin if l == 0 else X1
        with contextlib.ExitStack() as ph:
            gm = sb(ph, "gm", [128, KC, 2], F32)
            xt_pool = Pool([sb(ph, "xt%d" % i, [128, D], F32) for i in range(3)])
            junk = sb(ph, "junk", [128, D], BF16)
            ssp = Pool([sb(ph, "ss%d" % i, [128, 2], F32) for i in range(3)])
            k.op("dve", lambda e: e.tensor_scalar(out=gm.ap[:], in0=modT.ap[:, l, 16:32, :],
                                                  scalar1=1.0, scalar2=None, op0=ALU.add),
                 [modT], [gm])
            k.op("dve", lambda e: e.tensor_tensor(
                out=gm.ap[:], in0=gm.ap[:],
                in1=gTt.ap[:, l * KC:(l + 1) * KC].unsqueeze(2).to_broadcast([128, KC, 2]),
                op=ALU.mult), [gm, gTt], [gm])
            p1 = [dict() for _ in range(ntt)]

            def p1a(t):
                xt = xt_pool.next()
                ss = ssp.next()
                k.dma("sp", xt, xt.ap[:], xsrc, xsrc.ap[t * 128:(t + 1) * 128, :])
                k.op("act", lambda e: e.activation(out=junk.ap[:], in_=xt.ap[:], func=AF.Square,
                                                   accum_out=ss.ap[:, 0:1]), [xt], [junk, ss])
                rstd_ops(ss, 1.0 / D)
                k.op("dve", lambda e: e.tensor_scalar(out=xt.ap[:], in0=xt.ap[:], scalar1=ss.ap[:, 1:2],
                                                      scalar2=None, op0=ALU.mult), [xt, ss], [xt])
                p1[t]["xt"] = xt

            def p1b(t):
                r = 0 if t < 16 else 1
                xt = p1[t]["xt"]
                for q4 in range(4):
                    pt = PS.next()
                    for j in range(4):
                        kc = q4 * 4 + j
                        k.op("pe", lambda e, kc=kc, j=j: e.transpose(
                            out=pt.ap[:, j * 128:(j + 1) * 128], in_=xt.ap[:, kc * 128:(kc + 1) * 128],
                            identity=ident.ap[:]), [xt, ident], [pt], signal=(j == 3))
                    for j in range(4):
                        kc = q4 * 4 + j
                        dst = hT.ap[:, kc, t * 128:(t + 1) * 128]
                        src = pt.ap[:, j * 128:(j + 1) * 128]
                        if q4 < 2:
                            k.op("act", lambda e, dst=dst, src=src, kc=kc: e.activation(
                                out=dst, in_=src, func=AF.Identity,
                                scale=gm.ap[:, kc, r:r + 1], bias=modT.ap[:, l, kc, r:r + 1]),
                                [pt, gm, modT], [hTa])
                        else:
                            k.op("dve", lambda e, dst=dst, src=src, kc=kc: e.tensor_scalar(
                                out=dst, in0=src, scalar1=gm.ap[:, kc, r:r + 1],
                                scalar2=modT.ap[:, l, kc, r:r + 1], op0=ALU.mult, op1=ALU.add),
                                [pt, gm, modT], [hTb])

            _pipeline([p1a, p1b], [0, 1], ntt)
            if HTd is not None:
                for kc in range(KC):
                    k.dma("sp", HTd, HTd.ap[kc * 128:(kc + 1) * 128, :], hTa if kc < 8 else hTb, hT.ap[:, kc, :])
            k.barrier()
            k.release(ph)
        if stop_after == "1":
            lay12.close()
            break

        with contextlib.ExitStack() as ph:
            pbf_pool = Pool([sb(ph, "pbf%d" % i, [128, 512], BF16) for i in range(3)])
            deferred = []
            stg_pool = Pool([sb(ph, "stg%d" % i, [128, T], BF16) for i in range(3)])
            stv_pool = Pool([sb(ph, "stv%d" % i, [128, 512], BF16) for i in range(3)])
            ropeC = sb(ph, "ropeC", [128, TL], F32)
            ropeS = sb(ph, "ropeS", [128, TL], F32)
            t1p = Pool([sb(ph, "t1_%d" % i, [128, 512], F32) for i in range(2)])
            t2p = Pool([sb(ph, "t2_%d" % i, [128, 512], F32) for i in range(2)])
            k.dma("sp", ropeC, ropeC.ap[:], ropeC_in, ropeC_in.ap)
            k.dma("sp", ropeS, ropeS.ap[:], ropeS_in, ropeS_in.ap)

            lat_blocks = [(i * 512, 512) for i in range(4)]
            ctx_block = [(TL, TC)]
            specs = [
                (O_NAQ, 1024, "q", QT_na, not last),
                (O_NAK, 1024, "copy", KT_na, True),
                (O_NAV, 1024, "tok", V_na, True),
                (O_NAZ, 1024, "silu", ZT_na, not last),
                (O_RQ, 1024, "ropeq", QT_r, not last),
                (O_RK, 1024, "ropek", KT_r, True),
                (O_RV, 2048, "tok", V_r, True),
                (O_RZ, 2048, "silu", ZT_r, not last),
                (O_GNA, 4096, "sig", SG, not last),
            ]
            kscale = 128.0 ** -0.5
            ev = [0]
            for (c0, ncols, kind, dest, need_ctx) in specs:
                for cb in range(ncols // 512):
                    cc = c0 + cb * 512
                    if cc in pre_wt:
                        wt = pre_wt.pop(cc)
                    else:
                        wt = wt_pool.next()
                        k.dma("pool", wt, wt.ap[:], w_in,
                              w_in.ap[l, :, cc:cc + 512].rearrange("(k p) c -> p k c", p=128))
                    rope = kind in ("ropeq", "ropek")
                    if kind == "tok":
                        for t in range(NTT if need_ctx else 16):
                            pt = PS.next()
                            for kc in range(KC):
                                k.op("pe", lambda e, kc=kc, t=t: e.matmul(
                                    pt.ap[:], lhsT=hT.ap[:, kc, t * 128:(t + 1) * 128], rhs=wt.ap[:, kc, :],
                                    start=(kc == 0), stop=(kc == KC - 1)), [hTa, hTb, wt], [pt],
                                    signal=(kc == KC - 1))
                            st = stv_pool.next()
                            ev[0] += 1
                            if ev[0] % 2 or EV_ACT_ONLY:
                                k.op("act", lambda e: e.copy(out=st.ap[:], in_=pt.ap[:]), [pt], [st])
                            else:
                                k.op("dve", lambda e: e.tensor_copy(out=st.ap[:], in_=pt.ap[:]), [pt], [st])
                            k.dma("sp", dest, dest.ap[t * 128:(t + 1) * 128, cb * 512:(cb + 1) * 512],
                                  st, st.ap[:])
                        continue
                    blocks = lat_blocks + (ctx_block if need_ctx else [])
                    for j in range(4):
                        st = stg_pool.next()
                        for (t0, tn) in blocks:
                            is_ctx = t0 >= TL
                            pa = PS.next()
                            for kc in range(KC):
                                k.op("pe", lambda e, kc=kc: e.matmul(
                                    pa.ap[:, 0:tn], lhsT=wt.ap[:, kc, j * 128:(j + 1) * 128],
                                    rhs=hT.ap[:, kc, t0:t0 + tn], start=(kc == 0), stop=(kc == KC - 1)),
                                    [hTa, hTb, wt], [pa], signal=(kc == KC - 1))
                            dst = st.ap[:, t0:t0 + tn]
                            src = pa.ap[:, 0:tn]
                            if rope and not is_ctx:
                                pbf = pbf_pool.next()
                                k.op("act", lambda e, pbf=pbf, src=src, tn=tn: e.copy(out=pbf.ap[:, 0:tn], in_=src),
                                     [pa], [pbf])
                                sc_ = kscale if kind == "ropek" else 1.0

                                def fin(pa=pa, pbf=pbf, dst=dst, src=src, t0=t0, tn=tn, sc_=sc_, st=st):
                                    pb = PS.next()
                                    k.op("pe", lambda e: e.matmul(pb.ap[:, 0:tn], lhsT=permb.ap[:], rhs=pbf.ap[:, 0:tn],
                                                                  start=True, stop=True), [permb, pbf], [pb])
                                    t1 = t1p.next()
                                    t2 = t2p.next()
                                    k.op("dve", lambda e: e.scalar_tensor_tensor(
                                        out=t1.ap[:, 0:tn], in0=src, scalar=sc_, in1=ropeC.ap[:, t0:t0 + tn],
                                        op0=ALU.mult, op1=ALU.mult), [pa, ropeC, pbf], [t1])
                                    k.op("dve", lambda e: e.scalar_tensor_tensor(
                                        out=t2.ap[:, 0:tn], in0=pb.ap[:, 0:tn], scalar=sc_,
                                        in1=ropeS.ap[:, t0:t0 + tn], op0=ALU.mult, op1=ALU.mult),
                                        [pb, ropeS], [t2])
                                    k.op(POOL_ENG, lambda e: e.tensor_tensor(
                                        out=dst, in0=t1.ap[:, 0:tn], in1=t2.ap[:, 0:tn], op=ALU.add),
                                        [t1, t2], [st])
                                if deferred:
                                    deferred.pop()()
                                deferred.append(fin)
                            elif kind == "q":
                                k.op("act", lambda e: e.activation(out=dst, in_=src, func=AF.Copy,
                                                                   scale=kscale), [pa], [st])
                            elif kind == "ropek":
                                k.op("act", lambda e: e.activation(out=dst, in_=src, func=AF.Copy,
                                                                   scale=kscale), [pa], [st])
                            elif kind in ("copy", "ropeq"):
                                ev[0] += 1
                                if ev[0] % 2 or EV_ACT_ONLY:
                                    k.op("act", lambda e: e.copy(out=dst, in_=src), [pa], [st])
                                else:
                                    k.op("dve", lambda e: e.tensor_copy(out=dst, in_=src), [pa], [st])
                            elif kind == "silu":
                                k.op("act", lambda e: e.activation(out=dst, in_=src, func=AF.Silu), [pa], [st])
                            elif kind == "sig":
                                k.op("act", lambda e: e.activation(out=dst, in_=src, func=AF.Sigmoid), [pa], [st])
                        while deferred:
                            deferred.pop()()
                        tn_all = T if need_ctx else TL
                        r0 = cb * 512 + j * 128
                        k.dma("sp", dest, dest.ap[r0:r0 + 128, 0:tn_all], st, st.ap[:, 0:tn_all])
            k.barrier()
            k.release(ph)
        k.release(lay12)
        lay12.close()
        if stop_after == "2":
            break
        env = dict(nc=nc, k=k, sb=sb, reg=reg, PS=PS, m_tick=(m_tick if (defer_m and l == layers[0]) else None), l=l, last=last, rstd_ops=rstd_ops,
                   identb=identb, onesb=onesb, modT=modT)
        _phase_na(env, QT_na, KT_na, V_na, ZT_na, natab, AT)
        if defer_m and l == layers[0]:
            m_end(layers[1])
            k.barrier()
            k.release(mstack)
            mstack.close()
            PS.bufs = list(psb)
        if stop_after == "na":
            break
        pre3a = contextlib.ExitStack()
        wnap = Pool([sb(pre3a, "wna%d" % i, [128, 8, 512], BF16) for i in range(2)])
        wrep = Pool([sb(pre3a, "wre%d" % i, [128, 16, 512], BF16) for i in range(2)])
        pre_w = {}
        for cg in range(2):
            wna, wre = wnap.next(), wrep.next()
            k.dma("pool", wna, wna.ap[:], w_pna,
                  w_pna.ap[l, :, cg * 512:(cg + 1) * 512].rearrange("(k p) c -> p k c", p=128))
            k.dma("pool", wre, wre.ap[:], w_pret,
                  w_pret.ap[l, :, cg * 512:(cg + 1) * 512].rearrange("(k p) c -> p k c", p=128))
            pre_w[cg] = (wna, wre)
        env.update(wnap=wnap, wrep=wrep, pre_w=pre_w)
        _phase_ret(env, QT_r, KT_r, V_r, ZT_r, dlog, rpos_in, rcol_in, AT)
        if stop_after == "ret":
            break
        _phase_3a(env, AT, SG, w_pna, w_pret, MT)
        k.release(pre3a)
        pre3a.close()
        if stop_after == "3a":
            break
        _phase_3b(env, MT, w_out, GV, fg_in, xsrc, X1, yout)
        if stop_after == "3b":
            break

    if stop_after is not None:
        _finish(nc, k, yout, modT, dump)
    else:
        k.barrier()
        k.release(ph)


def _pipeline(stages, skews, n):
    for s in range(n + max(skews)):
        for fn, sk in zip(stages, skews):
            i = s - sk
            if 0 <= i < n:
                fn(i)


def _phase_na(env, QT_na, KT_na, V_na, ZT_na, natab, AT):
    k, sb, PS, l, last = env["k"], env["sb"], env["PS"], env["l"], env["last"]
    onesb = env["onesb"]
    nqb = 16 if last else 18
    tn_all = TL if last else T
    with contextlib.ExitStack() as ph:
        def mk(nm, shape, dt, n=2):
            return Pool([sb(ph, "%s%d" % (nm, i), shape, dt) for i in range(n)])
        qtp, ktp, ztp = mk("naq", [128, T], BF16), mk("nak", [128, T], BF16), mk("naz", [128, T], BF16)
        vtp = mk("nav", [128, NTT, 128], BF16)
        tabp = mk("natab", [128, NPAT, 128], F32)
        tabbp = mk("natabb", [128, NPAT, 128], BF16)
        identb = env["identb"]
        astp = mk("naa", [128, T], BF16)
        tmpp = mk("natmp", [128, 5 * 128], F32, 4)
        pexp = mk("napexp", [128, 7 * 128], BF16, 5)
        recp = mk("narec", [128, 128], F32, 3)
        ofp = mk("naof", [128, 128], F32, 3)
        hb = {}

        def load_head(h):
            qt, kt, zt, vt, tab, ast = (qtp.next(), ktp.next(), ztp.next(), vtp.next(),
                                        tabp.next(), astp.next())
            rs = slice(h * 128, (h + 1) * 128)
            k.dma("sp", kt, kt.ap[:], KT_na, KT_na.ap[rs, :])
            k.dma("sp", qt, qt.ap[:, 0:tn_all], QT_na, QT_na.ap[rs, 0:tn_all])
            k.dma("sp", zt, zt.ap[:, 0:tn_all], ZT_na, ZT_na.ap[rs, 0:tn_all])
            for g in range(3):
                k.dma("sp", vt, vt.ap[:, g * 6:(g + 1) * 6, :], V_na,
                      V_na.ap[g * 768:(g + 1) * 768, rs].rearrange("(t p) d -> p t d", p=128))
            k.dma("sp", tab, tab.ap[:], natab, natab.ap[l, h])
            tabb = tabbp.next()
            k.op("act", lambda e: e.copy(out=tabb.ap[:], in_=tab.ap[:]), [tab], [tabb])
            hb[h] = (qt, kt, zt, vt, tabb, ast)

        items = [(h, qb) for h in range(NH) for qb in range(nqb)]
        st = [dict() for _ in items]
        load_head(0)

        def tiles_of(qb):
            loc = [t for t in range(16) if (qb, t) in _PAT_OF] if qb < 16 else []
            return [16, 17] + loc, loc

        def s1(i):
            h, qb = items[i]
            if qb == 5 and h + 1 < NH:
                load_head(h + 1)
            if env.get("m_tick") is not None and i % 4 == 3 and i > 8:
                env["m_tick"](1)
            qt, kt, zt, vt, tab, ast = hb[h]
            qs = slice(qb * 128, (qb + 1) * 128)
            tiles, loc = tiles_of(qb)
            pA = PS.next()
            pB = PS.next() if len(loc) > 2 else None
            for j, t in enumerate(tiles):
                bank = pA if j < 4 else pB
                c = (j % 4) * 128
                local = j >= 2
                last_ = (j == 3 or j == len(tiles) - 1)
                k.op("pe", lambda e, bank=bank, c=c, t=t, local=local: e.matmul(
                    bank.ap[:, c:c + 128], lhsT=kt.ap[:, t * 128:(t + 1) * 128], rhs=qt.ap[:, qs],
                    start=True, stop=not local), [kt, qt], [bank],
                    signal=(last_ and not local))
                if local:
                    pat = _PAT_OF[(qb, t)]
                    k.op("pe", lambda e, bank=bank, c=c, pat=pat: e.matmul(
                        bank.ap[:, c:c + 128], lhsT=identb.ap[:], rhs=tab.ap[:, pat, :],
                        start=False, stop=True), [identb, tab], [bank], signal=last_)
            st[i].update(pA=pA, pB=pB)

        def s2(i):
            h, qb = items[i]
            qt, kt, zt, vt, tab, ast = hb[h]
            tiles, loc = tiles_of(qb)
            nl = len(loc)
            pA, pB = st[i]["pA"], st[i]["pB"]
            pe_ = pexp.next()
            na_ = min(4, 2 + nl) * 128
            k.op("act", lambda e: e.activation(out=pe_.ap[:, 0:na_], in_=pA.ap[:, 0:na_], func=AF.Exp),
                 [pA], [pe_])
            if nl > 2:
                nb_ = (nl - 2) * 128
                k.op("act", lambda e: e.activation(out=pe_.ap[:, 512:512 + nb_], in_=pB.ap[:, 0:nb_],
                                                   func=AF.Exp), [pB], [pe_])
            st[i]["pe"] = pe_

        def s3(i):
            h, qb = items[i]
            qt, kt, zt, vt, tab, ast = hb[h]
            tiles, loc = tiles_of(qb)
            pe_ = st[i]["pe"]
            pC = PS.next()
            nt = len(tiles)
            for j, t in enumerate(tiles):
                k.op("pe", lambda e, j=j, t=t: e.matmul(
                    pC.ap[:, 0:128], lhsT=vt.ap[:, t, :], rhs=pe_.ap[:, j * 128:(j + 1) * 128],
                    start=(j == 0), stop=(j == nt - 1)), [vt, pe_], [pC], signal=False)
            for j, t in enumerate(tiles):
                k.op("pe", lambda e, j=j: e.matmul(
                    pC.ap[:, 128:256], lhsT=onesb.ap[:], rhs=pe_.ap[:, j * 128:(j + 1) * 128],
                    start=(j == 0), stop=(j == nt - 1)), [onesb, pe_], [pC], signal=(j == nt - 1))
            st[i]["pC"] = pC

        def s4(i):
            h, qb = items[i]
            qt, kt, zt, vt, tab, ast = hb[h]
            qs = slice(qb * 128, (qb + 1) * 128)
            pC = st[i]["pC"]
            rec = recp.next()
            of = ofp.next()
            k.op("dve", lambda e: e.reciprocal(out=rec.ap[:], in_=pC.ap[:, 128:256]), [pC], [rec])
            k.op("dve", lambda e: e.tensor_tensor(out=of.ap[:], in0=pC.ap[:, 0:128], in1=rec.ap[:],
                                                  op=ALU.mult), [pC, rec], [of])
            k.op(POOL_ENG, lambda e: e.tensor_tensor(out=ast.ap[:, qs], in0=of.ap[:], in1=zt.ap[:, qs],
                                                   op=ALU.mult), [of, zt], [ast])
            if qb == nqb - 1:
                rs = slice(h * 128, (h + 1) * 128)
                k.dma("sp", AT, AT.ap[rs, 0:tn_all], ast, ast.ap[:, 0:tn_all])
            st[i].clear()

        _pipeline([s1, s2, s3, s4], NA_SKEW, len(items))
        if env.get("m_tick") is not None:
            env["m_tick"](64)
        k.barrier()
        k.release(ph)


def _phase_ret(env, QT_r, KT_r, V_r, ZT_r, dlog, rpos_in, rcol_in, AT):
    k, sb, PS, l, last = env["k"], env["sb"], env["PS"], env["l"], env["last"]
    identb, rstd_ops = env["identb"], env["rstd_ops"]
    tn_all = TL if last else T
    with contextlib.ExitStack() as ph:
        def mk(nm, shape, dt, n=2):
            return Pool([sb(ph, "%s%d" % (nm, i), shape, dt) for i in range(n)])
        rpos = sb(ph, "rpos", [128, 7, 128], F32)
        rcol = sb(ph, "rcol", [128, 2], F32)
        lg = sb(ph, "lg", [128, 16], F32)
        DT = sb(ph, "DT", [128, NH, 128], F32)
        DT2 = sb(ph, "DT2", [128, NH, 128], F32)
        qd = sb(ph, "qd", [128, NH, 2, 128], F32)
        kd = sb(ph, "kd", [128, NH, 2], F32)
        gd = sb(ph, "gd", [128, 16], F32)
        k.dma("sp", rpos, rpos.ap[:], rpos_in, rpos_in.ap)
        k.dma("sp", rcol, rcol.ap[:], rcol_in, rcol_in.ap)
        k.dma("sp", lg, lg.ap[:], dlog, dlog.ap[:, l * 16:(l + 1) * 16])
        k.op("act", lambda e: e.activation(out=lg.ap[:], in_=lg.ap[:], func=AF.Exp, scale=-1.0), [lg], [lg])
        k.op("act", lambda e: e.activation(out=lg.ap[:], in_=lg.ap[:], func=AF.Ln, bias=1.0), [lg], [lg])
        k.op("dve", lambda e: e.tensor_scalar(out=lg.ap[:], in0=lg.ap[:], scalar1=-1.0, scalar2=None,
                                              op0=ALU.mult), [lg], [lg])
        k.op("act", lambda e: e.activation(out=gd.ap[:], in_=lg.ap[:], func=AF.Exp, scale=128.0), [lg], [gd])
        for h in range(NH):
            lf, lb = lg.ap[:, h:h + 1], lg.ap[:, 8 + h:9 + h]
            k.op("act", lambda e: e.activation(out=DT.ap[:, h, :], in_=rpos.ap[:, 0, :], func=AF.Exp, scale=lf),
                 [rpos, lg], [DT])
            k.op("act", lambda e: e.activation(out=DT2.ap[:, h, :], in_=rpos.ap[:, 1, :], func=AF.Exp, scale=lb),
                 [rpos, lg], [DT2])
            k.op("act", lambda e: e.activation(out=qd.ap[:, h, 0, :], in_=rpos.ap[:, 5, :], func=AF.Exp, scale=lf),
                 [rpos, lg], [qd])
            k.op("act", lambda e: e.activation(out=qd.ap[:, h, 1, :], in_=rpos.ap[:, 6, :], func=AF.Exp, scale=lb),
                 [rpos, lg], [qd])
            k.op("act", lambda e: e.activation(out=kd.ap[:, h, 0:1], in_=rcol.ap[:, 0:1], func=AF.Exp, scale=lf),
                 [rcol, lg], [kd])
            k.op("act", lambda e: e.activation(out=kd.ap[:, h, 1:2], in_=rcol.ap[:, 1:2], func=AF.Exp, scale=lb),
                 [rcol, lg], [kd])
        bc = lambda a: a.unsqueeze(1).to_broadcast([128, NH, 128])
        k.op("dve", lambda e: e.tensor_tensor(out=DT.ap[:], in0=DT.ap[:], in1=bc(rpos.ap[:, 2, :]), op=ALU.mult),
             [DT, rpos], [DT])
        k.op("dve", lambda e: e.tensor_tensor(out=DT2.ap[:], in0=DT2.ap[:], in1=bc(rpos.ap[:, 3, :]), op=ALU.mult),
             [DT2, rpos], [DT2])
        k.op("dve", lambda e: e.tensor_tensor(out=DT.ap[:], in0=DT.ap[:], in1=DT2.ap[:], op=ALU.add),
             [DT, DT2], [DT])
        k.op("dve", lambda e: e.tensor_tensor(out=DT.ap[:], in0=DT.ap[:], in1=bc(rpos.ap[:, 4, :]), op=ALU.add),
             [DT, rpos], [DT])

        qtp, ktp = mk("rq", [128, T], BF16), mk("rk", [128, T], BF16)
        qfp, qbp = mk("rqf", [128, T], BF16, 1), mk("rqb", [128, T], BF16, 1)
        vp = mk("rv", [128, NTT, 256], BF16)
        zp = mk("rz", [128, 2, T], BF16)
        ap_ = mk("ra", [128, 2, T], BF16)
        kFp, kBp = mk("rkF", [128, NTT, 128], BF16, 1), mk("rkB", [128, NTT, 128], BF16, 1)
        Fbp, Bbp = mk("rFb", [128, NTT + 2, 256], BF16, 1), mk("rBb", [128, NTT + 2, 256], BF16, 1)
        Fm = mk("rFm", [128, 256], F32, 2)
        Bm = mk("rBm", [128, 256], F32, 2)
        sdp = mk("rsd", [128, 128], BF16, 6)
        onp = mk("ron", [128, 256], BF16, 8)
        jkp = mk("rjk", [128, 256], BF16, 4)
        ssp = mk("rss", [128, 2], F32, 8)

        hb = {}

        def load_head(h):
            qT, kT, v, z, a = qtp.next(), ktp.next(), vp.next(), zp.next(), ap_.next()
            rs = slice(h * 128, (h + 1) * 128)
            k.dma("sp", kT, kT.ap[:], KT_r, KT_r.ap[rs, :])
            k.dma("sp", qT, qT.ap[:, 0:tn_all], QT_r, QT_r.ap[rs, 0:tn_all])
            for g in range(3):
                k.dma("sp", v, v.ap[:, g * 6:(g + 1) * 6, :], V_r,
                      V_r.ap[g * 768:(g + 1) * 768, h * 256:(h + 1) * 256].rearrange("(t p) d -> p t d", p=128))
            for j in range(2):
                r0 = h * 256 + j * 128
                k.dma("sp", z, z.ap[:, j, 0:tn_all], ZT_r, ZT_r.ap[r0:r0 + 128, 0:tn_all])
            hb[h] = (qT, kT, v, z, a)

        load_head(0)
        for h in range(NH):
            qT, kT, v, z, a = hb[h]
            if h + 1 < NH:
                load_head(h + 1)
            qf, qb_ = qfp.next(), qbp.next()
            kF, kB, Fb, Bb = kFp.next(), kBp.next(), Fbp.next(), Bbp.next()
            ntq = tn_all // 128
            qv = lambda b: b.ap[:, 0:tn_all].rearrange("p (t c) -> p t c", c=128)
            k.op(POOL_ENG, lambda e: e.tensor_tensor(
                out=qv(qf), in0=qv(qT), in1=qd.ap[:, h, 0, :].unsqueeze(1).to_broadcast([128, ntq, 128]),
                op=ALU.mult), [qT, qd], [qf])
            k.op(POOL_ENG, lambda e: e.tensor_tensor(
                out=qv(qb_), in0=qv(qT), in1=qd.ap[:, h, 1, :].unsqueeze(1).to_broadcast([128, ntq, 128]),
                op=ALU.mult), [qT, qd], [qb_])
            for t in range(NTT):
                pk = PS.next()
                pkb = pk.ap[:].bitcast(BF16)
                k.op("pe", lambda e, t=t: e.transpose(out=pkb[:, 0:128], in_=kT.ap[:, t * 128:(t + 1) * 128],
                                                      identity=identb.ap[:]), [kT, identb], [pk])
                k.op("act", lambda e, t=t: e.activation(out=kF.ap[:, t, :], in_=pkb[:, 0:128], func=AF.Copy,
                                                        scale=kd.ap[:, h, 0:1]), [pk, kd], [kF])
                k.op("dve", lambda e, t=t: e.tensor_scalar(out=kB.ap[:, t, :], in0=pkb[:, 0:128],
                                                           scalar1=kd.ap[:, h, 1:2], scalar2=None,
                                                           op0=ALU.mult), [pk, kd], [kB])
            gF, gB = gd.ap[:, h:h + 1], gd.ap[:, 8 + h:9 + h]

            def run_seq(chunks, F0, Bend, need_out, slot0):
                n = len(chunks)
                Fcur = Fm.next()
                if F0 is None:
                    k.op("dve", lambda e: e.memset(Fcur.ap[:], 0.0), [], [Fcur])
                else:
                    k.op("dve", lambda e: e.tensor_copy(out=Fcur.ap[:], in_=F0.ap[:]), [F0], [Fcur])
                for i, t in enumerate(chunks):
                    k.op("act", lambda e, i=i: e.copy(out=Fb.ap[:, slot0 + i, :], in_=Fcur.ap[:]), [Fcur], [Fb])
                    pd = PS.next()
                    k.op("pe", lambda e, t=t: e.matmul(pd.ap[:, 0:256], lhsT=kF.ap[:, t, :], rhs=v.ap[:, t, :],
                                                       start=True, stop=True), [kF, v], [pd])
                    Fn = Fm.next()
                    k.op("dve", lambda e: e.scalar_tensor_tensor(
                        out=Fn.ap[:], in0=Fcur.ap[:], scalar=gF, in1=pd.ap[:, 0:256],
                        op0=ALU.mult, op1=ALU.add), [Fcur, pd, gd], [Fn])
                    Fcur = Fn
                Bcur = Bm.next()
                if Bend is None:
                    k.op("dve", lambda e: e.memset(Bcur.ap[:], 0.0), [], [Bcur])
                else:
                    k.op("dve", lambda e: e.tensor_copy(out=Bcur.ap[:], in_=Bend.ap[:]), [Bend], [Bcur])
                for i in range(n - 1, -1, -1):
                    t = chunks[i]
                    k.op("act", lambda e, i=i: e.copy(out=Bb.ap[:, slot0 + i, :], in_=Bcur.ap[:]), [Bcur], [Bb])
                    pd = PS.next()
                    k.op("pe", lambda e, t=t: e.matmul(pd.ap[:, 0:256], lhsT=kB.ap[:, t, :], rhs=v.ap[:, t, :],
                                                       start=True, stop=True), [kB, v], [pd])
                    Bn = Bm.next()
                    k.op("dve", lambda e: e.scalar_tensor_tensor(
                        out=Bn.ap[:], in0=Bcur.ap[:], scalar=gB, in1=pd.ap[:, 0:256],
                        op0=ALU.mult, op1=ALU.add), [Bcur, pd, gd], [Bn])
                    Bcur = Bn
                if need_out:
                    stt = [dict() for _ in chunks]

                    def r1a(i):
                        t = chunks[i]
                        ts_ = slice(t * 128, (t + 1) * 128)
                        pS = PS.next()
                        k.op("pe", lambda e: e.matmul(pS.ap[:, 0:128], lhsT=kT.ap[:, ts_], rhs=qT.ap[:, ts_],
                                                      start=True, stop=True), [kT, qT], [pS])
                        stt[i]["pS"] = pS

                    def r1b(i):
                        pS = stt[i]["pS"]
                        sd = sdp.next()
                        k.op("dve", lambda e: e.tensor_tensor(out=sd.ap[:], in0=pS.ap[:, 0:128],
                                                              in1=DT.ap[:, h, :], op=ALU.mult), [pS, DT], [sd])
                        stt[i]["sd"] = sd

                    def r2a(i):
                        t = chunks[i]
                        ts_ = slice(t * 128, (t + 1) * 128)
                        sd = stt[i]["sd"]
                        pO = PS.next()
                        k.op("pe", lambda e: e.matmul(pO.ap[:, 0:256], lhsT=sd.ap[:], rhs=v.ap[:, t, :],
                                                      start=True, stop=False), [sd, v], [pO], signal=False)
                        k.op("pe", lambda e: e.matmul(pO.ap[:, 0:256], lhsT=qf.ap[:, ts_], rhs=Fb.ap[:, slot0 + i, :],
                                                      start=False, stop=False), [qf, Fb], [pO], signal=False)
                        k.op("pe", lambda e: e.matmul(pO.ap[:, 0:256], lhsT=qb_.ap[:, ts_], rhs=Bb.ap[:, slot0 + i, :],
                                                      start=False, stop=True), [qb_, Bb], [pO])
                        stt[i]["pO"] = pO

                    def r2c(i):
                        pO = stt[i]["pO"]
                        ss, jk, on = ssp.next(), jkp.next(), onp.next()
                        k.op("act", lambda e: e.activation(out=jk.ap[:], in_=pO.ap[:, 0:256], func=AF.Square,
                                                           accum_out=ss.ap[:, 0:1]), [pO], [jk, ss])
                        rstd_ops(ss, 1.0 / 256)
                        stt[i].update(on=on, ss=ss)

                    def r2b(i):
                        on, pO, ss = stt[i]["on"], stt[i]["pO"], stt[i]["ss"]
                        k.op("act", lambda e: e.activation(out=on.ap[:], in_=pO.ap[:, 0:256], func=AF.Copy,
                                                           scale=ss.ap[:, 1:2]), [pO, ss], [on])

                    def r3a(i):
                        on = stt[i]["on"]
                        pT = PS.next()
                        pTb = pT.ap[:].bitcast(BF16)
                        for j in range(2):
                            k.op("pe", lambda e, j=j: e.transpose(
                                out=pTb[:, j * 128:(j + 1) * 128], in_=on.ap[:, j * 128:(j + 1) * 128],
                                identity=identb.ap[:]), [on, identb], [pT], signal=(j == 1))
                        stt[i]["pT"] = pT

                    def r3b(i):
                        t = chunks[i]
                        ts_ = slice(t * 128, (t + 1) * 128)
                        pT = stt[i]["pT"]
                        pTb = pT.ap[:].bitcast(BF16)
                        k.op("dve", lambda e: e.tensor_tensor(
                            out=a.ap[:, :, ts_], in0=pTb[:, 0:256].rearrange("p (j c) -> p j c", j=2),
                            in1=z.ap[:, :, ts_], op=ALU.mult), [pT, z], [a])
                        stt[i].clear()

                    _pipeline([r1a, r1b, r2a, r2c, r2b, r3a, r3b], [0, 1, 3, 4, 5, 7, 8], len(chunks))
                return Fcur, Bcur

            Fc, Bc = run_seq([16, 17], None, None, not last, 0)
            run_seq(list(range(16)), Fc, Bc, True, 2)
            for j in range(2):
                r0 = 1024 + h * 256 + j * 128
                k.dma("sp", AT, AT.ap[r0:r0 + 128, 0:tn_all], a, a.ap[:, j, 0:tn_all])
        k.barrier()
        k.release(ph)


def _phase_3a(env, AT, SG, w_pna, w_pret, MT):
    k, sb, PS, l, last = env["k"], env["sb"], env["PS"], env["l"], env["last"]
    tn_all = TL if last else T
    blocks = [(i * 512, 512) for i in range(4)] + ([] if last else [(TL, TC)])
    with contextlib.ExitStack() as ph:
        def mk(nm, shape, dt, n=2):
            return Pool([sb(ph, "%s%d" % (nm, i), shape, dt) for i in range(n)])
        a_all = sb(ph, "a_all", [128, 24, tn_all], BF16, coarse=True)
        a_blk = [env["reg"](Buf(a_all.ap, "a_blk%d" % i, coarse=True)) for i in range(len(blocks))]
        ph._bufs.extend(a_blk)

        def load_ablk(bi):
            t0, tn = blocks[bi]
            for c0 in (0, 8, 16):
                k.dma("sp", a_blk[bi], a_all.ap[:, c0:c0 + 8, t0:t0 + tn], AT,
                      AT.ap[c0 * 128:(c0 + 8) * 128, t0:t0 + tn].rearrange("(c p) t -> p c t", p=128),
                      sem_buf=a_blk[bi])
        load_ablk(0)
        wnap, wrep, pre_w = env["wnap"], env["wrep"], env["pre_w"]
        sgnp, sgrp = mk("sgn", [128, tn_all], BF16), mk("sgr", [128, tn_all], BF16)
        mstp = mk("mst", [128, tn_all], BF16)
        t1p, t2p = mk("m1", [128, 512], F32), mk("m2", [128, 512], F32)

        def load_sg(dc):
            sgn, sgr = sgnp.next(), sgrp.next()
            k.dma("sp", sgn, sgn.ap[:], SG, SG.ap[dc * 128:(dc + 1) * 128, 0:tn_all])
            k.dma("sp", sgr, sgr.ap[:], SG, SG.ap[2048 + dc * 128:2048 + (dc + 1) * 128, 0:tn_all])
            return sgn, sgr
        for cg in range(4):
            if cg in pre_w:
                wna, wre = pre_w[cg]
            else:
                wna, wre = wnap.next(), wrep.next()
                k.dma("pool", wna, wna.ap[:], w_pna,
                      w_pna.ap[l, :, cg * 512:(cg + 1) * 512].rearrange("(k p) c -> p k c", p=128))
                k.dma("pool", wre, wre.ap[:], w_pret,
                      w_pret.ap[l, :, cg * 512:(cg + 1) * 512].rearrange("(k p) c -> p k c", p=128))
            for j in range(4):
                dc = cg * 4 + j
                if dc == 0:
                    sg_next = load_sg(0)
                    for bi in range(1, len(blocks)):
                        load_ablk(bi)
                sgn, sgr = sg_next
                mst = mstp.next()
                if dc + 1 < 16:
                    sg_next = load_sg(dc + 1)
                for bi, (t0, tn) in enumerate(blocks):
                    ab = a_blk[bi]
                    pn, pr = PS.next(), PS.next()
                    for kc in range(8):
                        k.op("pe", lambda e, kc=kc: e.matmul(
                            pn.ap[:, 0:tn], lhsT=wna.ap[:, kc, j * 128:(j + 1) * 128],
                            rhs=a_all.ap[:, kc, t0:t0 + tn], start=(kc == 0), stop=(kc == 7)),
                            [wna, ab], [pn], signal=(kc == 7))
                    for kc in range(16):
                        k.op("pe", lambda e, kc=kc: e.matmul(
                            pr.ap[:, 0:tn], lhsT=wre.ap[:, kc, j * 128:(j + 1) * 128],
                            rhs=a_all.ap[:, 8 + kc, t0:t0 + tn], start=(kc == 0), stop=(kc == 15)),
                            [wre, ab], [pr], signal=(kc == 15))
                    t1, t2 = t1p.next(), t2p.next()
                    k.op("dve", lambda e: e.tensor_tensor(out=t1.ap[:, 0:tn], in0=pn.ap[:, 0:tn],
                                                          in1=sgn.ap[:, t0:t0 + tn], op=ALU.mult), [pn, sgn], [t1])
                    k.op("dve", lambda e: e.tensor_tensor(out=t2.ap[:, 0:tn], in0=pr.ap[:, 0:tn],
                                                          in1=sgr.ap[:, t0:t0 + tn], op=ALU.mult), [pr, sgr], [t2])
                    k.op(POOL_ENG, lambda e: e.tensor_tensor(out=mst.ap[:, t0:t0 + tn], in0=t1.ap[:, 0:tn],
                                                           in1=t2.ap[:, 0:tn], op=ALU.add), [t1, t2], [mst])
                k.dma("sp", MT, MT.ap[dc * 128:(dc + 1) * 128, 0:tn_all], mst, mst.ap[:])
        k.barrier()
        k.release(ph)


def _phase_3b(env, MT, w_out, GV, fg_in, xsrc, X1, yout):
    k, sb, PS, l, last = env["k"], env["sb"], env["PS"], env["l"], env["last"]
    rstd_ops = env["rstd_ops"]
    tn_all = TL if last else T
    with contextlib.ExitStack() as ph:
        def mk(nm, shape, dt, n=2):
            return Pool([sb(ph, "%s%d" % (nm, i), shape, dt) for i in range(n)])
        mt_all = sb(ph, "mt_all", [128, KC, tn_all], BF16, coarse=True)
        wo = sb(ph, "wo", [128, KC, D], BF16, coarse=True)
        nr = 1 if last else 2
        gate = sb(ph, "gate", [128, nr, D], F32, coarse=True)
        ngrp = (tn_all + 767) // 768
        mt_grp = [env["reg"](Buf(mt_all.ap, "mt_grp%d" % i, coarse=True)) for i in range(ngrp)]
        wo_cg = [env["reg"](Buf(wo.ap, "wo_cg%d" % i, coarse=True)) for i in range(4)]
        ph._bufs.extend(mt_grp + wo_cg)
        for cg in range(4):
            k.dma("pool", wo_cg[cg], wo.ap[:, :, cg * 512:(cg + 1) * 512], w_out,
                  w_out.ap[l, :, cg * 512:(cg + 1) * 512].rearrange("(k p) c -> p k c", p=128), sem_buf=wo_cg[cg])
        for g in range(ngrp):
            g0 = g * 768
            gn = min(768, tn_all - g0)
            for c0 in (0, 8):
                k.dma("sp", mt_grp[g], mt_all.ap[:, c0:c0 + 8, g0:g0 + gn], MT,
                      MT.ap[c0 * 128:(c0 + 8) * 128, g0:g0 + gn].rearrange("(c p) t -> p c t", p=128),
                      sem_buf=mt_grp[g])
        for r in range(nr):
            k.dma("sp", gate, gate.ap[:, r, :], GV, GV.ap[l * 2 + r:l * 2 + r + 1, :].partition_broadcast(128)
                  if False else GV.ap[l * 2 + r].partition_broadcast(128), sem_buf=gate)
        if last:
            fg = sb(ph, "fg", [128, D], F32)
            k.dma("sp", fg, fg.ap[:], fg_in, fg_in.ap[0].partition_broadcast(128))
            jk = sb(ph, "jk3", [128, D], BF16)
        xtp = mk("x3", [128, D], F32, 3)
        tmpp = mk("tmp3", [128, 512], F32, 2)
        ssp = mk("ss3", [128, 2], F32, 2)
        nt3 = tn_all // 128

        def load_x(t):
            xt = xtp.next()
            k.dma("sp", xt, xt.ap[:], xsrc, xsrc.ap[t * 128:(t + 1) * 128, :])
            return xt
        x_next = load_x(0)
        for t in range(nt3):
            r = 0 if t < 16 else 1
            xt = x_next
            if t + 1 < nt3:
                x_next = load_x(t + 1)
            for cg in range(4):
                cs = slice(cg * 512, (cg + 1) * 512)
                po = PS.next()
                for kc in range(KC):
                    k.op("pe", lambda e, kc=kc: e.matmul(
                        po.ap[:], lhsT=mt_all.ap[:, kc, t * 128:(t + 1) * 128], rhs=wo.ap[:, kc, cs],
                        start=(kc == 0), stop=(kc == KC - 1)), [mt_grp[(t * 128) // 768], wo_cg[cg]], [po],
                        signal=(kc == KC - 1))
                tmp = tmpp.next()
                k.op("dve", lambda e: e.tensor_tensor(out=tmp.ap[:], in0=po.ap[:], in1=gate.ap[:, r, cs],
                                                      op=ALU.mult), [po, gate], [tmp])
                k.op(POOL_ENG, lambda e: e.tensor_tensor(out=xt.ap[:, cs], in0=xt.ap[:, cs], in1=tmp.ap[:],
                                                       op=ALU.add), [xt, tmp], [xt])
            if last:
                ss = ssp.next()
                k.op("act", lambda e: e.activation(out=jk.ap[:], in_=xt.ap[:], func=AF.Square,
                                                   accum_out=ss.ap[:, 0:1]), [xt], [jk, ss])
                rstd_ops(ss, 1.0 / D)
                k.op("dve", lambda e: e.scalar_tensor_tensor(
                    out=xt.ap[:], in0=xt.ap[:], scalar=ss.ap[:, 1:2], in1=fg.ap[:],
                    op0=ALU.mult, op1=ALU.mult), [xt, ss, fg], [xt])
                k.dma("sp", yout, yout.ap[t * 128:(t + 1) * 128, :], xt, xt.ap[:])
            else:
                k.dma("sp", X1, X1.ap[t * 128:(t + 1) * 128, :], xt, xt.ap[:])
        k.barrier()
        k.release(ph)


def _finish(nc, k, yout, modT, dump):
    k.barrier()
    k.dma("sp", yout, yout.ap[0:128, 0:192], modT, modT.ap[:].rearrange("p l g r -> p (l g r)"),
          sem_buf=modT)
    k.barrier()


def _host_inputs(x, c, ctx, c_ctx, ada_w, ada_b, norm_g, w_in, na_rpb, ret_decay_logit,
                 w_proj_na, w_proj_ret, w_out, final_g):
    f = np.float32
    shared = {}
    shared["ada_w"] = np.ascontiguousarray(ada_w, f)
    shared["adabT"] = np.ascontiguousarray(
        np.asarray(ada_b, f).reshape(DEPTH, 48, 128).transpose(2, 0, 1).reshape(128, DEPTH * 48))
    shared["gT"] = np.ascontiguousarray(
        np.asarray(norm_g, f).reshape(DEPTH, KC, 128).transpose(2, 0, 1).reshape(128, DEPTH * KC))
    shared["w_in"] = np.ascontiguousarray(w_in, f)
    rpb = np.asarray(na_rpb, f)
    tab = rpb[:, :, _PDR, _PDC]
    tab = np.where(_PV[None, None], tab, f(NEG)).astype(f)
    shared["natab"] = np.ascontiguousarray(tab.transpose(0, 1, 3, 2, 4))
    dl = np.asarray(ret_decay_logit, f).reshape(1, DEPTH * 16)
    shared["dlog"] = np.ascontiguousarray(np.broadcast_to(dl, (128, DEPTH * 16)))
    shared["w_pna"] = np.ascontiguousarray(w_proj_na, f)
    shared["w_pret"] = np.ascontiguousarray(w_proj_ret, f)
    shared["w_out"] = np.ascontiguousarray(w_out, f)
    shared["final_g"] = np.ascontiguousarray(np.asarray(final_g, f).reshape(1, D))
    shared["ident"] = np.eye(128, dtype=f)
    shared["perm"] = np.ascontiguousarray(np.roll(np.eye(128, dtype=f), 64, axis=0))
    C, S = _rope_tables()
    shared["ropeC"] = C
    shared["ropeS"] = S
    tabs, cols = _ret_pos_tables()
    shared["rpos"] = tabs
    shared["rcol"] = cols
    maps = []
    cc = np.asarray(c_ctx, f).reshape(KC, 128).T
    for b in range(x.shape[0]):
        m = dict(shared)
        m["xin"] = np.ascontiguousarray(np.concatenate([np.asarray(x[b], f), np.asarray(ctx[b], f)], axis=0))
        cb = np.asarray(c[b], f).reshape(KC, 128).T
        m["cT"] = np.ascontiguousarray(np.concatenate([cb, cc], axis=1))
        maps.append(m)
    return maps


def kernel(**inputs):
    maps = _host_inputs(**inputs)
    nc = build_program()
    res = run_bass_kernel_spmd(nc, maps, core_ids=list(range(8)))
    return np.stack([r["yout"] for r in res.results], axis=0).astype(np.float32)
```
